# Optimizing a Trainium2 kernel written in Bass

```python
import math
import jax, jax.numpy as jnp
from jax import lax
import numpy as np

D_MODEL = 1024
BATCH = 8
SEQ = 8192
DEPTH = 2
DEC_BATCH = 8
DEC_SEQ = 32
PAST_LEN = 1024

CHUNK = 64
D_MIX = D_MODEL
CONV_DIM = D_MIX // 2
CONV_WIDTH = 3
N_HEADS = 8
HEAD_DIM = (D_MIX - CONV_DIM) // N_HEADS
N_KV_HEADS = 2
GROUP = N_HEADS // N_KV_HEADS
N_IDX_HEADS = 8
IDX_DIM = 64
TOPK_MAX = 256
N_BUCKETS = 32
REL_MAX_DIST = 128
D_FF = 4 * D_MODEL
Q_BLOCK = 128
ALPHA = (2 * DEPTH) ** 0.25
BETA = (8 * DEPTH) ** -0.25
LN_EPS = 1e-5
NEG = -1e30
SPLIT_SIZES = (CONV_DIM, CONV_DIM, CONV_DIM, N_HEADS * HEAD_DIM, N_KV_HEADS * HEAD_DIM,
               N_KV_HEADS * HEAD_DIM, N_IDX_HEADS * IDX_DIM, IDX_DIM, N_IDX_HEADS)
D_IN_PROJ = sum(SPLIT_SIZES)

kernel_name = "hybrid_conv_dsa_streaming_step"


def _split_points():
    pts, acc = [], 0
    for s in SPLIT_SIZES[:-1]:
        acc += s
        pts.append(acc)
    return pts


def _layernorm(x, g, b):
    xf = x.astype(jnp.float32)
    mu = jnp.mean(xf, axis=-1, keepdims=True)
    xc = xf - mu
    var = jnp.mean(xc * xc, axis=-1, keepdims=True)
    y = xc * lax.rsqrt(var + LN_EPS) * g.astype(jnp.float32) + b.astype(jnp.float32)
    return y.astype(x.dtype)


def _t5_bucket(rel):
    nb = N_BUCKETS // 2
    ret = (rel > 0).astype(jnp.int32) * nb
    n = jnp.abs(rel)
    max_exact = nb // 2
    nf = jnp.maximum(n, 1).astype(jnp.float32)
    large = max_exact + (jnp.log(nf / max_exact) / math.log(REL_MAX_DIST / max_exact)
                         * (nb - max_exact)).astype(jnp.int32)
    large = jnp.minimum(large, nb - 1)
    return ret + jnp.where(n < max_exact, n, large)


def _gather_rows(a, idx):
    return jax.vmap(lambda ab, ib: ab[ib])(a, idx)


def _sparse_attend(q, qi, wi, k, v, ki, q_pos, k_top, rel_bias):
    Bn, T = q.shape[0], q.shape[1]
    L = k.shape[1]
    f32 = jnp.float32
    s = jnp.einsum('btnd,bld->btnl', qi.astype(f32), ki.astype(f32)) * (IDX_DIM ** -0.5)
    score = jnp.einsum('btnl,btn->btl', jax.nn.relu(s), wi.astype(f32))
    limit = jnp.minimum((q_pos // CHUNK + 1) * CHUNK, L)
    admissible = jnp.arange(L, dtype=jnp.int32)[None, :] < limit[:, None]
    score = jnp.where(admissible[None], score, -jnp.inf)
    _, idx = lax.top_k(score, k_top)
    valid = idx < limit[None, :, None]
    k_sel = _gather_rows(k, idx).astype(f32)
    v_sel = _gather_rows(v, idx).astype(f32)
    qg = q.reshape(Bn, T, N_KV_HEADS, GROUP, HEAD_DIM).astype(f32)
    logits = jnp.einsum('btgrd,btkgd->btgrk', qg, k_sel) * (HEAD_DIM ** -0.5)
    bucket = _t5_bucket(idx - q_pos[None, :, None])
    bias = rel_bias.astype(f32)[bucket]
    bias = bias.reshape(Bn, T, k_top, N_KV_HEADS, GROUP).transpose(0, 1, 3, 4, 2)
    logits = jnp.where(valid[:, :, None, None, :], logits + bias, NEG)
    p = jax.nn.softmax(logits, axis=-1)
    out = jnp.einsum('btgrk,btkgd->btgrd', p, v_sel)
    return out.reshape(Bn, T, N_HEADS * HEAD_DIM).astype(q.dtype)


def _attention(q, qi, wi, k, v, ki, q_pos, k_top, rel_bias):
    Bn, T = q.shape[0], q.shape[1]
    if T > Q_BLOCK and T % Q_BLOCK == 0:
        nb = T // Q_BLOCK
        def blk(a):
            return jnp.swapaxes(a.reshape((Bn, nb, Q_BLOCK) + a.shape[2:]), 0, 1)
        xs = (blk(q), blk(qi), blk(wi), q_pos.reshape(nb, Q_BLOCK))
        out = lax.map(lambda t: _sparse_attend(t[0], t[1], t[2], k, v, ki, t[3], k_top, rel_bias), xs)
        return jnp.swapaxes(out, 0, 1).reshape(Bn, T, N_HEADS * HEAD_DIM)
    return _sparse_attend(q, qi, wi, k, v, ki, q_pos, k_top, rel_bias)


def _layer(x, conv_prev, k_past, v_past, ki_past, pos0,
           w_in, conv_w, w_o, ln1_g, ln1_b, w_ff1, w_ff2, ln2_g, ln2_b, rel_bias):
    Bn, T, _ = x.shape
    proj = x @ w_in
    gb, gc, h, q, k, v, qi, ki, wi = jnp.split(proj, _split_points(), axis=-1)
    u = gc * h
    u_pad = jnp.concatenate([conv_prev.astype(u.dtype), u], axis=1)
    y = conv_w[0] * u_pad[:, 0:T]
    for j in range(1, CONV_WIDTH):
        y = y + conv_w[j] * u_pad[:, j:j + T]
    conv_out = gb * y
    new_conv = u_pad[:, -(CONV_WIDTH - 1):]
    q = q.reshape(Bn, T, N_HEADS, HEAD_DIM)
    k = k.reshape(Bn, T, N_KV_HEADS, HEAD_DIM)
    v = v.reshape(Bn, T, N_KV_HEADS, HEAD_DIM)
    qi = qi.reshape(Bn, T, N_IDX_HEADS, IDX_DIM)
    wi = wi * (N_IDX_HEADS ** -0.5)
    k_all = jnp.concatenate([k_past.astype(k.dtype), k], axis=1)
    v_all = jnp.concatenate([v_past.astype(v.dtype), v], axis=1)
    ki_all = jnp.concatenate([ki_past.astype(ki.dtype), ki], axis=1)
    L = k_all.shape[1]
    k_top = min(TOPK_MAX, L // 4)
    q_pos = pos0 + jnp.arange(T, dtype=jnp.int32)
    attn = _attention(q, qi, wi, k_all, v_all, ki_all, q_pos, k_top, rel_bias)
    mix = jnp.concatenate([conv_out, attn], axis=-1) @ w_o
    x = _layernorm(ALPHA * x + mix, ln1_g, ln1_b)
    hid = jax.nn.relu(x @ w_ff1)
    ff = (hid * hid) @ w_ff2
    x = _layernorm(ALPHA * x + ff, ln2_g, ln2_b)
    return x, k, v, ki, new_conv


def setup_inputs(seed: int = 0) -> dict:
    key = jax.random.key(seed)
    ks = jax.random.split(key, 20)
    f32 = jnp.float32
    nrm = lambda k, shape, s: jax.random.normal(k, shape, f32) * s
    return {
        "x_prompt": nrm(ks[0], (BATCH, SEQ, D_MODEL), 1.0),
        "x_sample": nrm(ks[1], (DEC_BATCH, DEC_SEQ, D_MODEL), 1.0),
        "cache_k": nrm(ks[2], (DEPTH, DEC_BATCH, PAST_LEN, N_KV_HEADS, HEAD_DIM), 1.0),
        "cache_v": nrm(ks[3], (DEPTH, DEC_BATCH, PAST_LEN, N_KV_HEADS, HEAD_DIM), 1.0),
        "cache_kidx": nrm(ks[4], (DEPTH, DEC_BATCH, PAST_LEN, IDX_DIM), 1.0),
        "state_conv": nrm(ks[5], (DEPTH, DEC_BATCH, CONV_WIDTH - 1, CONV_DIM), 1.0),
        "w_in": nrm(ks[6], (DEPTH, D_MODEL, D_IN_PROJ), D_MODEL ** -0.5),
        "conv_w": nrm(ks[7], (DEPTH, CONV_WIDTH, CONV_DIM), CONV_WIDTH ** -0.5),
        "w_o": nrm(ks[8], (DEPTH, D_MIX, D_MODEL), BETA * D_MIX ** -0.5),
        "ln1_g": 1.0 + nrm(ks[9], (DEPTH, D_MODEL), 0.02),
        "ln1_b": nrm(ks[10], (DEPTH, D_MODEL), 0.02),
        "w_ff1": nrm(ks[11], (DEPTH, D_MODEL, D_FF), D_MODEL ** -0.5),
        "w_ff2": nrm(ks[12], (DEPTH, D_FF, D_MODEL), BETA * D_FF ** -0.5),
        "ln2_g": 1.0 + nrm(ks[13], (DEPTH, D_MODEL), 0.02),
        "ln2_b": nrm(ks[14], (DEPTH, D_MODEL), 0.02),
        "rel_bias": nrm(ks[15], (N_BUCKETS, N_HEADS), 0.1),
    }


def reference(x_prompt, x_sample, cache_k, cache_v, cache_kidx, state_conv,
              w_in, conv_w, w_o, ln1_g, ln1_b, w_ff1, w_ff2, ln2_g, ln2_b, rel_bias):
    hp, hs = x_prompt, x_sample
    Bp = x_prompt.shape[0]
    pk, pv, pki, pc, sk, sv, ski, sc = [], [], [], [], [], [], [], []
    for l in range(DEPTH):
        lw = (w_in[l], conv_w[l], w_o[l], ln1_g[l], ln1_b[l], w_ff1[l], w_ff2[l], ln2_g[l], ln2_b[l], rel_bias)
        zero_conv = jnp.zeros((Bp, CONV_WIDTH - 1, CONV_DIM), hp.dtype)
        empty_kv = jnp.zeros((Bp, 0, N_KV_HEADS, HEAD_DIM), hp.dtype)
        empty_ki = jnp.zeros((Bp, 0, IDX_DIM), hp.dtype)
        hp, k_n, v_n, ki_n, c_n = _layer(hp, zero_conv, empty_kv, empty_kv, empty_ki, 0, *lw)
        pk.append(k_n); pv.append(v_n); pki.append(ki_n); pc.append(c_n)
        hs, k_n, v_n, ki_n, c_n = _layer(hs, state_conv[l], cache_k[l], cache_v[l], cache_kidx[l], PAST_LEN, *lw)
        sk.append(k_n); sv.append(v_n); ski.append(ki_n); sc.append(c_n)
    return (hp, hs, jnp.stack(pk), jnp.stack(pv), jnp.stack(pki), jnp.stack(pc),
            jnp.stack(sk), jnp.stack(sv), jnp.stack(ski), jnp.stack(sc))
```

```python
import contextlib
import math
import numpy as np
import concourse.bass as bass
import concourse.mybir as mybir
from concourse.bass_utils import run_bass_kernel_spmd

F32 = mybir.dt.float32
BF16 = mybir.dt.bfloat16
ALU = mybir.AluOpType
AF = mybir.ActivationFunctionType
AX = mybir.AxisListType

D = 1024
DEPTH = 2
ALPHA = (2 * DEPTH) ** 0.25
LN_EPS = 1e-5
NIT = 18
NSLOT = 3
MASKV = -30000.0


class Prog:
    ENG = ("pe", "act", "dve", "pool", "sp")

    def __init__(self, nc):
        self.nc = nc
        self.ops = []
        self.last_w = {}
        self.readers = {}
        self.chan_cnt = {}

    def add(self, eng, fn, reads=(), writes=(), chan=None):
        idx = len(self.ops)
        ps_reads = [r for r in reads if isinstance(r, tuple) and r[0] == "ps"]
        if ps_reads:
            reads = [r for r in reads if r not in ps_reads]
            writes = list(writes) + ps_reads
        deps = set()
        for r in reads:
            w = self.last_w.get(r)
            if w is not None:
                deps.add(w)
        for r in writes:
            w = self.last_w.get(r)
            if w is not None:
                deps.add(w)
            for rd in self.readers.get(r, ()):
                deps.add(rd)
        for r in reads:
            self.readers.setdefault(r, []).append(idx)
        for r in writes:
            self.last_w[r] = idx
            self.readers[r] = []
        deps.discard(idx)
        op = dict(eng=eng, fn=fn, deps=deps, chan=chan, need_inc=False)
        if chan is not None:
            self.chan_cnt[chan] = self.chan_cnt.get(chan, 0) + 16
            op["cval"] = self.chan_cnt[chan]
        self.ops.append(op)
        return idx

    def pe(self, fn, reads=(), writes=()):
        return self.add("pe", fn, reads, writes)

    def act(self, fn, reads=(), writes=()):
        return self.add("act", fn, reads, writes)

    def dve(self, fn, reads=(), writes=()):
        return self.add("dve", fn, reads, writes)

    def pool(self, fn, reads=(), writes=()):
        return self.add("pool", fn, reads, writes)

    def dma(self, eng, chan, fn, reads=(), writes=()):
        return self.add(eng, fn, reads, writes, chan=chan)

    def emit(self):
        nc = self.nc
        ops = self.ops
        for op in ops:
            for d in op["deps"]:
                p = ops[d]
                if p["chan"] is None:
                    if p["eng"] == "pe" and op["eng"] == "pe" and op["chan"] is None:
                        continue
                    p["need_inc"] = True
        tick = {e: 0 for e in self.ENG}
        for op in ops:
            if op["chan"] is None and op["need_inc"]:
                tick[op["eng"]] += 1
                op["tick"] = tick[op["eng"]]
        with contextlib.ExitStack() as st:
            esem = {e: st.enter_context(nc.semaphore("s_" + e)) for e in self.ENG}
            csem = {c: st.enter_context(nc.semaphore("c_%s" % (str(c),)))
                    for c in self.chan_cnt}
            block = st.enter_context(nc.Block())
            engobj = {"pe": "tensor", "act": "scalar", "dve": "vector",
                      "pool": "gpsimd", "sp": "sync"}
            per_eng = {e: [] for e in self.ENG}
            for op in ops:
                per_eng[op["eng"]].append(op)
            for e in self.ENG:
                lst = per_eng[e]

                def body(eobj, lst=lst, e=e):
                    seen = {}
                    for op in lst:
                        waits = {}
                        for d in op["deps"]:
                            p = ops[d]
                            if p["chan"] is not None:
                                key = ("c", p["chan"])
                                val = p["cval"]
                            else:
                                if p["eng"] == "pe" and e == "pe" and op["chan"] is None:
                                    continue
                                key = ("e", p["eng"])
                                val = p["tick"]
                            if val > waits.get(key, 0):
                                waits[key] = val
                        for key, val in waits.items():
                            if seen.get(key, 0) >= val:
                                continue
                            seen[key] = val
                            sem = csem[key[1]] if key[0] == "c" else esem[key[1]]
                            eobj.wait_ge(sem, val)
                        ins = op["fn"](eobj)
                        if op["chan"] is not None:
                            ins.then_inc(csem[op["chan"]], 16)
                        elif op["need_inc"]:
                            ins.then_inc(esem[e], 1)
                    if e == "sp":
                        for c, v in self.chan_cnt.items():
                            if seen.get(("c", c), 0) < v:
                                eobj.wait_ge(csem[c], v)

                getattr(block, engobj[e])(body)


def _bucket_np(rel):
    nb = 16
    ret = (rel > 0).astype(np.int32) * nb
    n = np.abs(rel)
    me = 8
    nf = np.maximum(n, 1).astype(np.float32)
    large = me + (np.log(nf / np.float32(me)) / np.float32(math.log(128 / me))
                  * np.float32(nb - me)).astype(np.int32)
    large = np.minimum(large, nb - 1)
    return ret + np.where(n < me, n, large)


def _static_consts():
    c = {}
    c["ident"] = np.eye(128, dtype=np.float32)
    c["antiid"] = np.ascontiguousarray(np.eye(128, dtype=np.float32)[::-1])
    j = np.arange(512)
    rel = 255 - j
    b = _bucket_np(rel.astype(np.int32))
    oh = np.zeros((32, 512), np.float32)
    oh[b, j] = 1.0
    oh[15, :] -= 1.0
    oh[:, 511] = 0.0
    c["ohvec"] = oh
    sel = np.zeros((8, 4, 128), np.float32)
    for jj in range(4):
        sel[2 * jj, jj, 0:64] = 1.0
        sel[2 * jj + 1, jj, 64:128] = 1.0
    c["sel"] = sel.reshape(8, 512)
    k = np.arange(NIT + 2)
    c["pow2"] = np.ascontiguousarray(np.broadcast_to((0.5 ** (k + 1)).astype(np.float32), (128, NIT + 2)))
    return c


_IN_COLS = None


def _in_cols():
    cols = []
    cols += list(range(0, 512))
    cols += list(range(512, 1024))
    cols += list(range(1024, 1536))
    cols += list(range(2880, 2888)) + [-1] * 120
    for jj in range(4):
        cols += list(range(1536 + 64 * jj, 1536 + 64 * jj + 64))
        cols += list(range(1536 + 64 * (4 + jj), 1536 + 64 * (4 + jj) + 64))
    cols += list(range(2048, 2176))
    cols += list(range(2176, 2304))
    cols += list(range(2816, 2880)) * 2
    cols += list(range(2304, 2816))
    return np.array(cols)


def _tile_w(wm, nct, nkc):
    return wm.reshape(nkc, 128, nct, 128).transpose(2, 1, 0, 3)


def _prep_weights(w_in, w_o, w_ff1, w_ff2):
    cols = _in_cols()
    out = np.zeros((DEPTH, 24, 128, 4096), np.float32)
    rows_o = []
    for kc in range(8):
        for p in range(128):
            if kc < 4:
                rows_o.append(128 * kc + p)
            else:
                jj = kc - 4
                rows_o.append(512 + 64 * jj + p if p < 64 else 512 + 64 * (4 + jj) + (p - 64))
    rows_o = np.array(rows_o)
    for l in range(DEPTH):
        wi = np.zeros((1024, 3072), np.float32)
        valid = cols >= 0
        wi[:, valid] = w_in[l][:, cols[valid]]
        t = _tile_w(wi, 24, 8)
        out[l, 0:6] = t.reshape(6, 4, 128, 1024).transpose(0, 2, 1, 3).reshape(6, 128, 4096)
        t = _tile_w(w_o[l][rows_o, :], 8, 8)
        out[l, 6:8] = t.reshape(2, 4, 128, 1024).transpose(0, 2, 1, 3).reshape(2, 128, 4096)
        t = _tile_w(w_ff1[l], 32, 8)
        out[l, 8:16] = t.reshape(8, 4, 128, 1024).transpose(0, 2, 1, 3).reshape(8, 128, 4096)
        t = _tile_w(w_ff2[l], 8, 32)
        out[l, 16:24] = t.reshape(8, 128, 4096)
    return out


def build(SEQ=8192, do_sample=True, nlayers=DEPTH):
    nc = bass.Bass("TRN2", target_bir_lowering=False)
    NT = SEQ // 512
    NST = SEQ // 128

    def din(name, shape, dt=F32):
        return nc.dram_tensor(name, shape, dt, kind="ExternalInput").ap()

    def dout(name, shape, dt=F32):
        return nc.dram_tensor(name, shape, dt, kind="ExternalOutput").ap()

    xp = din("xp", [SEQ, D])
    xs = din("xs", [32, D])
    ck = din("ck", [DEPTH, 1024, 128])
    cv = din("cv", [DEPTH, 1024, 128])
    cki = din("cki", [DEPTH, 1024, 64])
    scv = din("scv", [DEPTH, 2, 512])
    wall = din("wall", [DEPTH, 24, 128, 4096])
    lnp = din("lnp", [128, DEPTH * 4 * 8])
    cwp = din("cwp", [128, DEPTH * 3 * 4])
    rbd = din("rb", [32, 8])
    d_ident = din("ident", [128, 128])
    d_anti = din("antiid", [128, 128])
    d_oh = din("ohvec", [32, 512])
    d_sel = din("sel", [8, 512])
    d_pow2 = din("pow2", [128, NIT + 2])

    y = dout("y", [SEQ, D])
    ys = dout("ys", [32, D])
    pk = dout("pk", [DEPTH, SEQ, 128])
    pv = dout("pv", [DEPTH, SEQ, 128])
    pki = dout("pki", [DEPTH, SEQ, 64])
    pc = dout("pc", [DEPTH, 2, 512])
    sk = dout("sk", [DEPTH, 32, 128])
    sv = dout("sv", [DEPTH, 32, 128])
    ski = dout("ski", [DEPTH, 32, 64])
    sc = dout("sc", [DEPTH, 2, 512])

    wb = nc.dram_tensor("wb", [DEPTH, 24, 128, 4096], BF16).ap()
    x1 = nc.dram_tensor("x1", [NT + 1, 128, 8 * 512], F32).ap()
    fd = nc.dram_tensor("fd", [8, 512], F32).ap()

    st = contextlib.ExitStack()
    with st:
        def sb(name, shape, dt):
            return st.enter_context(nc.sbuf_tensor(name, shape, dt))

        def psum(name, shape, dt):
            return st.enter_context(nc.psum_tensor(name, shape, dt))

        KT = sb("KT", [128, SEQ], BF16)
        VA = sb("VA", [128, NST, 2, 65], BF16)
        KiT = sb("KiT", [128, SEQ], BF16)
        KTs = sb("KTs", [128, 1056], BF16)
        VAs = sb("VAs", [128, 9, 2, 65], BF16)
        KiTs = sb("KiTs", [128, 1056], BF16)
        arena = sb("arena", [128, 8192], F32)
        xT = arena[:, 0:4096].rearrange("p (c t) -> p c t", c=8)
        xb = arena[:, 4096:6144].bitcast(BF16).rearrange("p (c t) -> p c t", c=8)
        big = sb("big", [128, 8192], F32)
        mneg = sb("mneg", [128, 8192], BF16)
        wbuf = [arena[:, 6144:8192].bitcast(BF16)] + [sb("wbuf%d" % k, [128, 4096], BF16)[:, :] for k in range(1, NSLOT)]
        ARENA_RES = [("xT", c) for c in range(8)] + [("xb", c) for c in range(8)] + [("w", 0)]
        qT4 = sb("qT4", [128, 4, 512], BF16)
        qiT = sb("qiT", [128, 4, 512], BF16)
        convT = sb("convT", [128, 4, 512], BF16)
        attnT = sb("attnT", [128, 4, 512], BF16)
        PT = [sb("PT%d" % k, [128, 512], BF16) for k in range(2)]
        uT = sb("uT", [128, 4, 516], F32)
        stg = [sb("stg%d" % k, [128, 1024], F32) for k in range(2)]
        lnm = sb("lnm", [128, 512], F32)
        lnr = sb("lnr", [128, 512], F32)
        Rh = [sb("Rh%d" % k, [128, 512], BF16) for k in range(4)]
        Dh = sb("Dh", [128, 8, 128], BF16)
        Qz = [sb("Qz%d" % g, [128, 4, 128], BF16) for g in range(2)]
        tmpO = sb("tmpO", [128, 512], F32)
        rec = sb("rec", [128, 512], F32)
        sgnT = sb("sgnT", [128, 4, 8], F32)
        biasT = sb("biasT", [128, 2, 2, 4, 128], BF16)
        ident = sb("identf", [128, 128], F32)
        identb = sb("identb", [128, 128], BF16)
        I4 = sb("I4", [128, 4, 128], BF16)
        onesb = sb("onesb", [128, 128], BF16)
        onesf = sb("onesf", [128, 64], F32)
        rbs = sb("rbs", [32, 8], F32)
        pow2 = sb("pow2s", [128, NIT + 2], F32)
        lnps = sb("lnps", [128, DEPTH * 32], F32)
        cws = sb("cws", [128, DEPTH * 12], F32)
        bis = sb("bis", [128, 64], F32)
        PS = [psum("ps%d" % k, [128, 512], F32) for k in range(8)]

        hid = big[:].bitcast(BF16)
        lnn = rec
        antib = Rh[3][:, 0:128]
        lnt = tmpO
        wsT = lnm
        wabsT = lnr
        rtmp = [Rh[0], Rh[1]]
        ohv = lnm
        zb = mneg[:, 0:4096]
        zsq = mneg[:, 4096:8192]

        P = Prog(nc)
        rr_state = [0]

        def rr():
            rr_state[0] = (rr_state[0] + 1) % 8
            return rr_state[0]

        def cload(dst, src):
            P.dma("sp", "cst", lambda e: e.dma_start(out=dst, in_=src), writes=["consts"])
        cload(ident[:], d_ident)
        cload(tmpO[:, 0:128], d_anti)
        cload(lnm[0:32, :], d_oh)
        cload(rbs[:], rbd)
        cload(pow2[:], d_pow2)
        cload(lnps[:], lnp)
        cload(cws[:], cwp)
        P.dve(lambda e: e.tensor_copy(out=identb[:], in_=ident[:]), reads=["consts"], writes=["identb"])
        P.dve(lambda e: e.tensor_copy(out=antib[:], in_=tmpO[:, 0:128]), reads=["consts"], writes=[("Rh", 3), "tmpO"])
        for k in range(4):
            P.dve(lambda e, k=k: e.tensor_copy(out=I4[:, k, :], in_=ident[:]), reads=["consts"], writes=["I4"])
        P.pool(lambda e: e.memset(onesb[:], 1.0 / 1024.0), writes=["onesb"])
        P.pool(lambda e: e.memset(onesf[:], 1.0), writes=["onesf"])
        for g in range(2):
            P.pool(lambda e, g=g: e.memset(Qz[g][:], 0.0), writes=[("Qz", g)])
        P.pool(lambda e: e.memset(VA[:, :, :, 64:65], 1.0), writes=["VA"])
        P.pool(lambda e: e.memset(VAs[:, :, :, 64:65], 1.0), writes=["VAs"])

        P.pe(lambda e: e.matmul(PS[0][0:8, :], lhsT=rbs[:, :], rhs=lnm[0:32, :], start=True, stop=True),
             reads=["consts", "lnm"], writes=[("ps", 0)])
        P.act(lambda e: e.activation(out=rec[0:8, :], in_=PS[0][0:8, :], func=AF.Copy),
              reads=[("ps", 0)], writes=["rec"])
        P.dma("sp", "cst", lambda e: e.dma_start(out=fd, in_=rec[0:8, :]), reads=["rec"], writes=["fd", "consts"])
        for ty in range(2):
            off = 128 + 128 * ty
            src = bass.AP(fd.tensor, off, [[1, 128], [512, 8], [1, 128]])
            P.dma("sp", "cst", lambda e, src=src: e.dma_start(out=big[:, 0:1024].rearrange("p (h t) -> p h t", h=8), in_=src),
                  reads=["fd"], writes=["big", "consts"])
            P.dve(lambda e: e.tensor_copy(out=mneg[:, 0:1024], in_=big[:, 0:1024]), reads=["big"], writes=["mneg"])
            for g in range(2):
                b = 1 + g
                P.pe(lambda e, b=b, g=g: e.matmul(PS[b][:, :], lhsT=antib[:, :], rhs=mneg[:, 512 * g:512 * g + 512],
                                                 start=True, stop=True),
                     reads=["mneg", ("Rh", 3)], writes=[("ps", b)])
                P.act(lambda e, b=b, g=g, ty=ty: e.activation(
                    out=biasT[:, ty, g, :, :].rearrange("p h t -> p (h t)"), in_=PS[b][:, :], func=AF.Copy),
                    reads=[("ps", b)], writes=["biasT"])

        cast_state = {"emitted": set()}

        def emit_cast(l, s):
            if (l, s) in cast_state["emitted"]:
                return
            cast_state["emitted"].add((l, s))
            P.dma("pool", "wc%d_%d" % (l, s), lambda e: e.dma_start(out=wb[l, s], in_=wall[l, s]),
                  writes=[("wbd", l, s)])

        for s in range(_DBG.get("ncast", 24)):
            emit_cast(0, s)

        wseq = []
        wstate = {"emitted": 0}

        def w_emit_upto(n):
            if wstate.get("cap") is not None:
                n = min(n, wstate["cap"])
            while wstate["emitted"] <= min(n, len(wseq) - 1):
                i = wstate["emitted"]
                l, s = wseq[i]
                k = i % NSLOT
                P.dma("sp", "w%d" % k, lambda e, l=l, s=s, k=k: e.dma_start(out=wbuf[k][:, :], in_=wb[l, s]),
                      reads=[("wbd", l, s)], writes=[("w", k)])
                wstate["emitted"] += 1

        wcur = {"n": 0}

        def wslot():
            n = wcur["n"]
            wcur["n"] += 1
            w_emit_upto(n + NSLOT - 1)
            k = n % NSLOT
            return wbuf[k], ("w", k)

        def evac(i, out, in_, reads, writes, scale=None):
            if i % 2 == 0:
                if scale is None:
                    P.act(lambda e: e.activation(out=out, in_=in_, func=AF.Copy), reads=reads, writes=writes)
                else:
                    P.act(lambda e: e.activation(out=out, in_=in_, func=AF.Copy, scale=scale), reads=reads, writes=writes)
            else:
                if scale is None:
                    P.dve(lambda e: e.tensor_copy(out=out, in_=in_), reads=reads, writes=writes)
                else:
                    P.dve(lambda e: e.tensor_scalar(out=out, in0=in_, scalar1=scale, scalar2=None, op0=ALU.mult),
                          reads=reads, writes=writes)

        def proj_tile(wt, wres, ctl, T, rhs_fn, nkc, reads):
            b = rr()
            for kc in range(nkc):
                P.pe(lambda e, kc=kc, b=b: e.matmul(PS[b][:, 0:T], lhsT=wt[:, (ctl * nkc + kc) * 128:(ctl * nkc + kc) * 128 + 128],
                                                   rhs=rhs_fn(kc), start=(kc == 0), stop=(kc == nkc - 1)),
                     reads=[wres] + reads, writes=[("ps", b)])
            return b

        def ln_apply(l, which, T):
            bm, be = rr(), rr()
            for c in range(8):
                P.pe(lambda e, c=c: e.matmul(PS[bm][:, 0:T], lhsT=onesb[:, :], rhs=zb[:, c * 512:c * 512 + T],
                                             start=(c == 0), stop=(c == 7)), reads=["mneg", "onesb"], writes=[("ps", bm)])
            for c in range(8):
                P.pe(lambda e, c=c: e.matmul(PS[be][:, 0:T], lhsT=onesb[:, :], rhs=zsq[:, c * 512:c * 512 + T],
                                             start=(c == 0), stop=(c == 7)), reads=["mneg", "onesb"], writes=[("ps", be)])
            P.act(lambda e: e.activation(out=lnm[:, 0:T], in_=PS[bm][:, 0:T], func=AF.Copy), reads=[("ps", bm)], writes=["lnm"])
            P.pool(lambda e: e.tensor_tensor(out=lnt[:, 0:T], in0=lnm[:, 0:T], in1=lnm[:, 0:T], op=ALU.mult),
                   reads=["lnm"], writes=["tmpO"])
            P.dve(lambda e: e.tensor_tensor(out=lnr[:, 0:T], in0=PS[be][:, 0:T], in1=lnt[:, 0:T], op=ALU.subtract),
                  reads=[("ps", be), "tmpO"], writes=["lnr"])
            P.dve(lambda e: e.tensor_scalar(out=lnr[:, 0:T], in0=lnr[:, 0:T], scalar1=LN_EPS, scalar2=None, op0=ALU.add),
                  reads=["lnr"], writes=["lnr"])
            P.act(lambda e: e.activation(out=lnt[:, 0:T], in_=lnr[:, 0:T], func=AF.Sqrt), reads=["lnr"], writes=["tmpO"])
            P.dve(lambda e: e.reciprocal(out=lnr[:, 0:T], in_=lnt[:, 0:T]), reads=["tmpO"], writes=["lnr"])
            P.dve(lambda e: e.scalar_tensor_tensor(out=lnn[:, 0:T], in0=lnm[:, 0:T], scalar=-1.0, in1=lnr[:, 0:T],
                                                   op0=ALU.mult, op1=ALU.mult), reads=["lnm", "lnr"], writes=["rec"])
            gcol = (l * 4 + 2 * which) * 8
            bcol = (l * 4 + 2 * which + 1) * 8
            for c in range(8):
                P.pool(lambda e, c=c: e.tensor_tensor(out=xT[:, c, 0:T], in0=xT[:, c, 0:T], in1=lnr[:, 0:T], op=ALU.mult),
                       reads=[("xT", c), "lnr"], writes=[("xT", c)])
                P.dve(lambda e, c=c: e.tensor_tensor(out=xT[:, c, 0:T], in0=xT[:, c, 0:T], in1=lnn[:, 0:T], op=ALU.add),
                      reads=[("xT", c), "rec"], writes=[("xT", c)])
                P.dve(lambda e, c=c: e.tensor_scalar(out=xT[:, c, 0:T], in0=xT[:, c, 0:T], scalar1=lnps[:, gcol + c:gcol + c + 1],
                                                     scalar2=lnps[:, bcol + c:bcol + c + 1], op0=ALU.mult, op1=ALU.add),
                      reads=[("xT", c), "consts"], writes=[("xT", c)])
                P.act(lambda e, c=c: e.activation(out=xb[:, c, 0:T], in_=xT[:, c, 0:T], func=AF.Copy),
                      reads=[("xT", c)], writes=[("xb", c)])

        def resid_evac(b, ct, T):
            P.dve(lambda e: e.scalar_tensor_tensor(out=xT[:, ct, 0:T], in0=xT[:, ct, 0:T], scalar=ALPHA, in1=PS[b][:, 0:T],
                                                   op0=ALU.mult, op1=ALU.add),
                  reads=[("xT", ct), ("ps", b)], writes=[("xT", ct)])
            P.act(lambda e: e.activation(out=zb[:, ct * 512:ct * 512 + T], in_=xT[:, ct, 0:T], func=AF.Copy),
                  reads=[("xT", ct)], writes=["mneg"])
            P.act(lambda e: e.activation(out=zsq[:, ct * 512:ct * 512 + T], in_=xT[:, ct, 0:T], func=AF.Square),
                  reads=[("xT", ct)], writes=["mneg"])

        def process_tile(l, kind, ti):
            samp = (kind == "s")
            T = 32 if samp else 512
            QS = 32 if samp else 128
            NS = 1 if samp else 4
            pos0 = 1024 if samp else 512 * ti
            kt, va, kit = (KTs, VAs, KiTs) if samp else (KT, VA, KiT)
            ktn, van, kitn = ("KTs", "VAs", "KiTs") if samp else ("KT", "VA", "KiT")
            x1i = NT if samp else ti

            if l == 0:
                src = xs if samp else xp
                for j in range(NS):
                    sg = stg[j % 2]
                    sgn = ("stg", j % 2)
                    r0 = j * QS if samp else 512 * ti + j * QS
                    P.dma("sp", "stg%d" % (j % 2), lambda e, sg=sg, r0=r0: e.dma_start(out=sg[0:QS, :], in_=src[r0:r0 + QS, :]),
                          writes=[sgn])
                    for half in range(2):
                        b = rr()
                        for c4 in range(4):
                            c = half * 4 + c4
                            P.pe(lambda e, c=c, c4=c4, b=b, sg=sg: e.matmul(PS[b][:, c4 * 128:c4 * 128 + QS], lhsT=sg[0:QS, c * 128:(c + 1) * 128], rhs=ident[0:QS, 0:QS], start=True, stop=True),
                                 reads=[sgn, "consts"], writes=[("ps", b)])
                        o3 = xT[:, half * 4:half * 4 + 4, j * QS:(j + 1) * QS]
                        ob = xb[:, half * 4:half * 4 + 4, j * QS:(j + 1) * QS]
                        i3 = PS[b][:, :].rearrange("p (c t) -> p c t", c=4)[:, :, 0:QS]
                        wr = [("xT", half * 4 + c4) for c4 in range(4)]
                        wrb = [("xb", half * 4 + c4) for c4 in range(4)]
                        if not _DBG.get("s0_noact"):
                            P.act(lambda e, o3=o3, i3=i3: e.activation(out=o3, in_=i3, func=AF.Copy), reads=[("ps", b)], writes=wr)
                        if not _DBG.get("s0_nodve"):
                            P.dve(lambda e, ob=ob, i3=i3: e.tensor_copy(out=ob, in_=i3), reads=[("ps", b)], writes=wrb)
            else:
                if samp:
                    P.dma("sp", "x1l", lambda e: e.dma_start(out=xT[:, :, 0:T], in_=x1[x1i].rearrange("p (c t) -> p c t", c=8)[:, :, 0:T]),
                          reads=[("x1", x1i)], writes=[("xT", c) for c in range(8)])
                else:
                    P.dma("sp", "x1l", lambda e: e.dma_start(out=arena[:, 0:4096], in_=x1[x1i]),
                          reads=[("x1", x1i)], writes=[("xT", c) for c in range(8)])
                for c in range(8):
                    evac(c, xb[:, c, 0:T], xT[:, c, 0:T], [("xT", c)], [("xb", c)])

            if _DBG["stop"] <= -3:
                return
            if samp:
                P.dma("sp", "stg0", lambda e: e.dma_start(out=stg[0][:, :].rearrange("p (n d) -> p n d", n=8),
                                                          in_=ck[l].rearrange("(n p) d -> p n d", p=128)), writes=[("stg", 0)])
                for half in range(2):
                    b = rr()
                    for n4 in range(4):
                        n = half * 4 + n4
                        P.pe(lambda e, n=n, n4=n4, b=b: e.matmul(PS[b][:, n4 * 128:(n4 + 1) * 128], lhsT=stg[0][:, n * 128:(n + 1) * 128], rhs=ident[:, :], start=True, stop=True),
                             reads=[("stg", 0), "consts"], writes=[("ps", b)])
                    evac(half, KTs[:, half * 512:half * 512 + 512], PS[b][:, :], [("ps", b)], ["KTs"])
                if _DBG["stop"] <= -2:
                    return
                P.dma("sp", "stg1", lambda e: e.dma_start(out=stg[1][:, :].rearrange("p (n d) -> p n d", n=8),
                                                          in_=cv[l].rearrange("(n p) d -> p n d", p=128)), writes=[("stg", 1)])
                P.dve(lambda e: e.tensor_copy(out=VAs[:, 0:8, :, 0:64],
                                              in_=stg[1][:, :].rearrange("p (n g d) -> p n g d", n=8, g=2)),
                      reads=[("stg", 1)], writes=["VAs"])
                if _DBG["stop"] <= -1:
                    return
                P.dma("sp", "stg0", lambda e: e.dma_start(out=stg[0][:, 0:512].rearrange("p (n d) -> p n d", n=8),
                                                          in_=cki[l].rearrange("(n p) d -> p n d", p=128)), writes=[("stg", 0)])
                for half in range(2):
                    b = rr()
                    for n4 in range(4):
                        n = half * 4 + n4
                        P.pe(lambda e, n=n, n4=n4, b=b: e.matmul(PS[b][0:64, n4 * 128:(n4 + 1) * 128], lhsT=stg[0][:, n * 64:(n + 1) * 64], rhs=ident[:, :], start=True, stop=True),
                             reads=[("stg", 0), "consts"], writes=[("ps", b)])
                    P.act(lambda e, half=half, b=b: e.activation(out=KiTs[0:64, half * 512:half * 512 + 512], in_=PS[b][0:64, :], func=AF.Copy),
                          reads=[("ps", b)], writes=["KiTs"])
                    P.dve(lambda e, half=half, b=b: e.tensor_copy(out=KiTs[64:128, half * 512:half * 512 + 512], in_=PS[b][0:64, :]),
                          reads=[("ps", b)], writes=["KiTs"])

            if _DBG["stop"] <= 0:
                return
            use_arena = (not samp)
            if use_arena:
                wstate["cap"] = wcur["n"] + 5
                if l == 0:
                    P.dma("pool", "x1s", lambda e: e.dma_start(out=x1[x1i], in_=arena[:, 0:4096]),
                          reads=[("xT", c) for c in range(8)], writes=[("x1", x1i)])
            xb_reads = [("xb", c) for c in range(8)]

            def xrhs(kc):
                return xb[:, kc, 0:T]

            gb = big[:, 0:2048].rearrange("p (c t) -> p c t", c=4)
            gc = big[:, 2048:4096].rearrange("p (c t) -> p c t", c=4)
            hh = big[:, 4096:6144].rearrange("p (c t) -> p c t", c=4)
            for s in range(3):
                wt, wres = wslot()
                for ctl in range(4):
                    ct = s * 4 + ctl
                    b = proj_tile(wt, wres, ctl, T, xrhs, 8, xb_reads)
                    evac(ct, big[:, ct * 512:ct * 512 + T], PS[b][:, 0:T], [("ps", b)], ["big"])
            if samp:
                for c in range(4):
                    P.dma("sp", "x1l", lambda e, c=c: e.dma_start(out=uT[:, c, 0:2], in_=scv[l][:, c * 128:(c + 1) * 128].rearrange("j p -> p j"),
                                                                  allow_slow_non_contiguous=True), writes=["uT"])
            elif ti == 0:
                P.pool(lambda e: e.memset(uT[:, :, 0:2], 0.0), writes=["uT"])
            else:
                P.pool(lambda e: e.tensor_copy(out=uT[:, :, 0:2], in_=uT[:, :, 512:514]), reads=["uT"], writes=["uT"])
            P.pool(lambda e: e.tensor_tensor(out=uT[:, :, 2:2 + T], in0=gc[:, :, 0:T], in1=hh[:, :, 0:T], op=ALU.mult),
                   reads=["big"], writes=["uT"])
            for c in range(4):
                w0 = cws[:, (l * 3 + 0) * 4 + c:(l * 3 + 0) * 4 + c + 1]
                w1 = cws[:, (l * 3 + 1) * 4 + c:(l * 3 + 1) * 4 + c + 1]
                w2 = cws[:, (l * 3 + 2) * 4 + c:(l * 3 + 2) * 4 + c + 1]
                yy = gc[:, c, 0:T]
                P.dve(lambda e, c=c, w0=w0, yy=yy: e.tensor_scalar(out=yy, in0=uT[:, c, 0:T], scalar1=w0, scalar2=None, op0=ALU.mult),
                      reads=["uT", "consts"], writes=["big"])
                P.dve(lambda e, c=c, w1=w1, yy=yy: e.scalar_tensor_tensor(out=yy, in0=uT[:, c, 1:1 + T], scalar=w1, in1=yy,
                                                                         op0=ALU.mult, op1=ALU.add), reads=["uT", "big"], writes=["big"])
                P.dve(lambda e, c=c, w2=w2, yy=yy: e.scalar_tensor_tensor(out=yy, in0=uT[:, c, 2:2 + T], scalar=w2, in1=yy,
                                                                         op0=ALU.mult, op1=ALU.add), reads=["uT", "big"], writes=["big"])
                P.pool(lambda e, c=c, yy=yy: e.tensor_tensor(out=convT[:, c, 0:T], in0=gb[:, c, 0:T], in1=yy, op=ALU.mult),
                       reads=["big"], writes=["convT"])
            if samp or ti == NT - 1:
                dst = sc if samp else pc
                for c in range(4):
                    P.dma("pool", "pcv", lambda e, c=c, dst=dst: e.dma_start(out=dst[l][:, c * 128:(c + 1) * 128].rearrange("j p -> p j"), in_=uT[:, c, T:T + 2],
                                                                    allow_slow_non_contiguous=True), reads=["uT"], writes=["pc_out"])

            if _DBG["stop"] <= 1:
                return
            kvst = big[:, 6144:7680].rearrange("p (n t) -> p n t", n=3)
            wt, wres = wslot()
            b = proj_tile(wt, wres, 0, T, xrhs, 8, xb_reads)
            wsc = 1.0 / (8.0 * math.sqrt(8.0))
            P.act(lambda e, b=b: e.activation(out=wsT[0:8, 0:T], in_=PS[b][0:8, 0:T], func=AF.Copy, scale=wsc),
                  reads=[("ps", b)], writes=["lnm"])
            bt = rr()
            for j in range(NS):
                P.pe(lambda e, j=j: e.matmul(PS[bt][0:QS, j * 8:j * 8 + 8], lhsT=wsT[0:8, j * QS:(j + 1) * QS], rhs=ident[0:8, 0:8], start=True, stop=True),
                     reads=["lnm", "consts"], writes=[("ps", bt)])
            P.act(lambda e: e.activation(out=sgnT[0:QS, 0:NS, :], in_=PS[bt][0:QS, 0:NS * 8].rearrange("p (n h) -> p n h", h=8),
                                         func=AF.Copy), reads=[("ps", bt)], writes=["sgnT"])
            for jj in range(4):
                if jj == 3:
                    wt, wres = wslot()
                ctl = (1 + jj) % 4
                b = proj_tile(wt, wres, ctl, T, xrhs, 8, xb_reads)
                evac(jj, qT4[:, jj, 0:T], PS[b][:, 0:T], [("ps", b)], ["qT4"], scale=0.125)
            b = proj_tile(wt, wres, 1, T, xrhs, 8, xb_reads)
            P.act(lambda e, b=b: e.activation(out=kvst[:, 0, 0:T], in_=PS[b][:, 0:T], func=AF.Copy), reads=[("ps", b)], writes=["big"])
            P.dve(lambda e, b=b: e.tensor_copy(out=kt[:, pos0:pos0 + T], in_=PS[b][:, 0:T]), reads=[("ps", b)], writes=[ktn])
            b = proj_tile(wt, wres, 2, T, xrhs, 8, xb_reads)
            P.act(lambda e, b=b: e.activation(out=kvst[:, 1, 0:T], in_=PS[b][:, 0:T], func=AF.Copy), reads=[("ps", b)], writes=["big"])
            b = proj_tile(wt, wres, 3, T, xrhs, 8, xb_reads)
            P.act(lambda e, b=b: e.activation(out=kvst[:, 2, 0:T], in_=PS[b][:, 0:T], func=AF.Copy), reads=[("ps", b)], writes=["big"])
            P.dve(lambda e, b=b: e.tensor_copy(out=kit[:, pos0:pos0 + T], in_=PS[b][:, 0:T]), reads=[("ps", b)], writes=[kitn])
            for which in range(3):
                wd = 64 if which == 2 else 128
                b = rr()
                for j in range(NS):
                    P.pe(lambda e, j=j, which=which, wd=wd, b=b: e.matmul(PS[b][0:QS, j * 128:j * 128 + wd], lhsT=kvst[0:wd, which, j * QS:(j + 1) * QS], rhs=ident[0:wd, 0:wd], start=True, stop=True),
                         reads=["big", "consts"], writes=[("ps", b)])
                sg = stg[which % 2]
                sgn = ("stg", which % 2)
                src3 = PS[b][0:QS, :].rearrange("p (n d) -> p n d", n=4)[:, 0:NS, 0:wd]
                dst3 = sg[0:QS, 0:512].rearrange("p (n d) -> p n d", n=4)[:, 0:NS, 0:wd]
                P.act(lambda e, src3=src3, dst3=dst3: e.activation(out=dst3, in_=src3, func=AF.Copy), reads=[("ps", b)], writes=[sgn])
                if which == 1:
                    if samp:
                        vdst = VAs[0:32, 8:9, :, 0:64]
                    else:
                        vdst = VA[:, 4 * ti:4 * ti + 4, :, 0:64]
                    vsrc = PS[b][0:QS, :].rearrange("p (n g d) -> p n g d", n=4, g=2)[:, 0:NS, :, :]
                    P.dve(lambda e, vdst=vdst, vsrc=vsrc: e.tensor_copy(out=vdst, in_=vsrc), reads=[("ps", b)], writes=[van])
                if samp:
                    od = [sk, sv, ski][which][l]
                    P.dma("pool", "so%d" % (which % 2), lambda e, od=od, sg=sg, wd=wd: e.dma_start(out=od, in_=sg[0:32, 0:wd]),
                          reads=[sgn], writes=["kv_out"])
                else:
                    od = [pk, pv, pki][which][l, pos0:pos0 + 512, :].rearrange("(n p) d -> p n d", p=128)
                    P.dma("pool", "so%d" % (which % 2), lambda e, od=od, dst3=dst3: e.dma_start(out=od, in_=dst3),
                          reads=[sgn], writes=["kv_out"])
            wt, wres = wslot()
            for jj in range(4):
                b = proj_tile(wt, wres, jj, T, xrhs, 8, xb_reads)
                evac(jj, qiT[:, jj, 0:T], PS[b][:, 0:T], [("ps", b)], ["qiT"])

            if _DBG["stop"] <= 2:
                return
            NW = 4 * QS

            def Lof(j):
                return 1056 if samp else pos0 + QS * (j + 1)

            def scb(j):
                if use_arena and (j % 2 == 1):
                    return arena, ARENA_RES
                return big, ["big"]

            def idx_steps(j, overlapped=False):
                L = Lof(j)
                nblk = (L + 511) // 512
                steps = []
                SC, SCR = scb(j)

                def s_diag():
                    for h in range(8):
                        P.dve(lambda e, h=h: e.tensor_scalar(out=Dh[0:QS, h, 0:QS], in0=identb[0:QS, 0:QS],
                                                             scalar1=sgnT[0:QS, j, h:h + 1], scalar2=None, op0=ALU.mult),
                              reads=["identb", "sgnT"], writes=["Dh"])
                if overlapped:
                    s_diag()
                else:
                    steps.append(s_diag)
                pairs = [(blk, hp) for blk in range(nblk) for hp in range(4)]

                SB = [(0, 1), (2, 3), (5, 6)]
                RB = [(Rh[0], ("Rh", 0), Rh[1], ("Rh", 1)), (Rh[2], ("Rh", 2), Rh[3], ("Rh", 3)),
                      (PT[0], ("PT", 0), PT[1], ("PT", 1))]

                def emit_A(q):
                    blk, hp = pairs[q]
                    c0 = blk * 512
                    wd = min(512, L - c0)
                    for h2 in range(2):
                        h = 2 * hp + h2
                        pr = 64 * h2
                        bk = SB[q % 3][h2]
                        P.pe(lambda e, h=h, pr=pr, bk=bk: e.matmul(
                            PS[bk][0:QS, 0:wd], lhsT=qiT[pr:pr + 64, h // 2, j * QS:(j + 1) * QS],
                            rhs=kit[pr:pr + 64, c0:c0 + wd], start=True, stop=True),
                            reads=["qiT", kitn], writes=[("ps", bk)])

                def emit_B(q):
                    blk, hp = pairs[q]
                    wd = min(512, L - blk * 512)
                    for h2 in range(2):
                        bk = SB[q % 3][h2]
                        rt, rn = RB[q % 3][2 * h2], RB[q % 3][2 * h2 + 1]
                        if h2 == 0 or overlapped:
                            P.act(lambda e, bk=bk, rt=rt: e.activation(out=rt[0:QS, 0:wd], in_=PS[bk][0:QS, 0:wd], func=AF.Relu),
                                  reads=[("ps", bk)], writes=[rn])
                        else:
                            P.dve(lambda e, bk=bk, rt=rt: e.tensor_scalar(out=rt[0:QS, 0:wd], in0=PS[bk][0:QS, 0:wd], scalar1=0.0,
                                                                          scalar2=None, op0=ALU.max),
                                  reads=[("ps", bk)], writes=[rn])

                def emit_C(q):
                    blk, hp = pairs[q]
                    c0 = blk * 512
                    wd = min(512, L - c0)
                    for h2 in range(2):
                        h = 2 * hp + h2
                        rt, rn = RB[q % 3][2 * h2], RB[q % 3][2 * h2 + 1]
                        P.pe(lambda e, h=h, rt=rt: e.matmul(PS[4][0:QS, 0:wd], lhsT=Dh[0:QS, h, 0:QS],
                                                            rhs=rt[0:QS, 0:wd], start=(h == 0), stop=(h == 7)),
                             reads=["Dh", rn], writes=[("ps", 4)])
                    if hp == 3:
                        P.act(lambda e: e.activation(out=SC[0:QS, c0:c0 + wd], in_=PS[4][0:QS, 0:wd], func=AF.Copy),
                              reads=[("ps", 4)], writes=SCR)

                nq = len(pairs)
                for q in range(nq + 2):
                    def s_q(q=q):
                        if q < nq:
                            emit_A(q)
                        if 2 <= q:
                            emit_C(q - 2)
                        if q < nq:
                            emit_B(q)
                    steps.append(s_q)
                return steps

            def bis_steps(j, k0=None, share=0.35, kend=None):
                L = Lof(j)
                steps = []
                SC, SCR = scb(j)
                need_bis = samp or (L > 256)
                THR = bis[0:QS, 60:61]
                if need_bis:
                    MX, MN, RNG, MID, TMP, T1 = (bis[0:QS, k:k + 1] for k in (0, 1, 2, 3, 4, 5))
                    WT = bis[0:QS, 8:8 + NIT + 2]
                    CNT = bis[0:QS, 32:32 + NIT + 1]
                    CNA = bis[0:QS, 51:52]
                    nA = (int(L * share) // 64) * 64
                    if k0 is None or nA < 64 or L - nA < 64:
                        k0 = NIT
                    La = L - nA

                    def stride0(col, n):
                        x = bis[0:QS, col:col + 1]
                        return bass.AP(x.tensor, x.offset, [list(x.ap[0]), [0, n]])
                    jd = stride0(62, L)
                    jda = stride0(62, La)
                    ja = stride0(61, nA)

                    def s_pre():
                        P.dve(lambda e: e.tensor_reduce(out=MX, in_=SC[0:QS, 0:L], axis=AX.X, op=ALU.max, apply_absolute_value=True),
                              reads=SCR, writes=["bis"])
                        if not samp:
                            P.pool(lambda e: e.memset(SC[0:64, L - 64:L], -1e30), reads=["bis"], writes=SCR)
                        P.dve(lambda e: e.tensor_scalar(out=MN, in0=MX, scalar1=-1.0, scalar2=None, op0=ALU.mult), reads=["bis"], writes=["bis"])
                        P.dve(lambda e: e.tensor_scalar(out=RNG, in0=MX, scalar1=2.0, scalar2=1e-6, op0=ALU.mult, op1=ALU.add),
                              reads=["bis"], writes=["bis"])
                        P.dve(lambda e: e.tensor_scalar(out=WT, in0=pow2[0:QS, :], scalar1=RNG, scalar2=None, op0=ALU.mult),
                              reads=["bis", "consts"], writes=["bis"])
                        P.dve(lambda e: e.tensor_tensor(out=MID, in0=MN, in1=WT[:, 0:1], op=ALU.add), reads=["bis"], writes=["b_mid"])
                        P.dve(lambda e: e.memset(CNT, 0.0), reads=["b_cnt"], writes=["b_cnt"])
                    steps.append(s_pre)
                    for k in range(NIT):
                        def s_it(k=k):
                            if k0 <= k < (NIT if kend is None else kend):
                                P.act(lambda e: e.activation(out=ja, in_=SC[0:QS, La:L], func=AF.Sign, bias=MID, scale=-1.0,
                                                             accum_out=CNA), reads=SCR + ["b_mid"], writes=["b_cna"])
                                P.dve(lambda e: e.tensor_scalar(out=jda, in0=SC[0:QS, 0:La], scalar1=MID, scalar2=0.0,
                                                                op0=ALU.is_ge, op1=ALU.add, accum_out=CNT[:, k:k + 1]),
                                      reads=SCR + ["b_mid"], writes=["b_cnt"])
                                P.dve(lambda e: e.scalar_tensor_tensor(out=T1, in0=CNT[:, k:k + 1], scalar=2.0, in1=CNA,
                                                                       op0=ALU.mult, op1=ALU.subtract), reads=["b_cnt", "b_cna"], writes=["b_tmp"])
                                P.dve(lambda e: e.tensor_scalar(out=TMP, in0=T1, scalar1=float(512 - nA), scalar2=0.5,
                                                                op0=ALU.is_ge, op1=ALU.subtract), reads=["b_tmp"], writes=["b_tmp"])
                            else:
                                P.dve(lambda e: e.tensor_scalar(out=jd, in0=SC[0:QS, 0:L], scalar1=MID, scalar2=0.0,
                                                                op0=ALU.is_ge, op1=ALU.add, accum_out=CNT[:, k:k + 1]),
                                      reads=SCR + ["b_mid"], writes=["b_cnt"])
                                P.dve(lambda e: e.tensor_scalar(out=TMP, in0=CNT[:, k:k + 1], scalar1=256.0, scalar2=0.5,
                                                                op0=ALU.is_ge, op1=ALU.subtract), reads=["b_cnt"], writes=["b_tmp"])
                            P.dve(lambda e: e.scalar_tensor_tensor(out=MID, in0=TMP, scalar=WT[:, k:k + 1], in1=MID,
                                                                   op0=ALU.mult, op1=ALU.add), reads=["b_tmp", "bis"], writes=["b_mid"])
                        steps.append(s_it)

                    def s_thr():
                        P.dve(lambda e: e.tensor_tensor(out=THR, in0=MID, in1=WT[:, NIT:NIT + 1], op=ALU.subtract),
                              reads=["b_mid", "bis"], writes=["bis"])
                    steps.append(s_thr)
                else:
                    def s_thr0():
                        P.pool(lambda e: e.memset(SC[0:64, L - 64:L], -1e30), writes=SCR)
                        P.dve(lambda e: e.memset(THR, -1e29), reads=["bis"], writes=["bis"])
                    steps.append(s_thr0)
                return steps

            def do_mask(j):
                for g in range(2):
                    P.pool(lambda e, g=g: e.tensor_copy(out=Qz[g][64 * g:64 * g + 64, :, 0:QS],
                                                        in_=qT4[64 * g:64 * g + 64, :, j * QS:(j + 1) * QS]),
                           reads=["qT4"], writes=[("Qz", g)])
                L = Lof(j)
                SC, SCR = scb(j)
                THR = bis[0:QS, 60:61]
                P.dve(lambda e: e.tensor_scalar(out=mneg[0:QS, 0:L], in0=SC[0:QS, 0:L], scalar1=THR, scalar2=MASKV,
                                                op0=ALU.is_lt, op1=ALU.mult), reads=SCR + ["bis"], writes=["mneg"])

            def att_steps(j):
                L = Lof(j)
                nst = (L + 127) // 128
                steps = []
                units = [(g, s_) for g in range(2) for s_ in range(nst)]

                def ks_of(s_):
                    return min(128, L - 128 * s_)

                TB = [5, 6, 0]
                OB = [7, 3]
                PB = [(PT[0], ("PT", 0)), (PT[1], ("PT", 1)), (Rh[0], ("Rh", 0))]

                def emit_Q(u):
                    g, s_ = units[u]
                    ks = ks_of(s_)
                    bk = TB[u % 3]
                    if samp:
                        ty = 0 if s_ == 8 else (1 if s_ == 7 else None)
                    else:
                        ty = 0 if s_ == nst - 1 else (1 if s_ == nst - 2 else None)
                    P.pe(lambda e: e.matmul(
                        PS[bk][0:ks, 0:NW], lhsT=kt[:, 128 * s_:128 * s_ + ks],
                        rhs=Qz[g][:, :, 0:QS], start=True, stop=False),
                        reads=[ktn, ("Qz", g)], writes=[("ps", bk)])
                    P.pe(lambda e: e.matmul(
                        PS[bk][0:ks, 0:NW], lhsT=mneg[0:QS, 128 * s_:128 * s_ + ks], rhs=I4[0:QS, :, 0:QS],
                        start=False, stop=(ty is None)), reads=["mneg", "I4"], writes=[("ps", bk)])
                    if ty is not None:
                        P.pe(lambda e: e.matmul(
                            PS[bk][0:ks, 0:NW], lhsT=identb[0:ks, 0:ks], rhs=biasT[0:ks, ty, g, :, 0:QS],
                            start=False, stop=True), reads=["identb", "biasT"], writes=[("ps", bk)])

                def emit_X(u):
                    g, s_ = units[u]
                    ks = ks_of(s_)
                    bk = TB[u % 3]
                    pt, ptn = PB[u % 3]
                    P.act(lambda e: e.activation(out=pt[0:ks, 0:NW], in_=PS[bk][0:ks, 0:NW], func=AF.Exp),
                          reads=[("ps", bk)], writes=[ptn])

                def emit_V(u):
                    g, s_ = units[u]
                    ks = ks_of(s_)
                    pt, ptn = PB[u % 3]
                    ob = OB[g]
                    P.pe(lambda e: e.matmul(
                        PS[ob][0:65, 0:NW], lhsT=va[0:ks, s_, g, :], rhs=pt[0:ks, 0:NW], start=(s_ == 0), stop=(s_ == nst - 1)),
                        reads=[van, ptn], writes=[("ps", ob)])

                def emit_fin(g):
                    ob = OB[g]
                    bk = (2, 5)[g]
                    P.dve(lambda e: e.reciprocal(out=rec[64:65, 0:NW], in_=PS[ob][64:65, 0:NW]), reads=[("ps", ob)], writes=["rec"])
                    P.act(lambda e: e.activation(out=tmpO[0:64, 0:NW], in_=PS[ob][0:64, 0:NW], func=AF.Copy),
                          reads=[("ps", ob)], writes=["tmpO"])
                    P.pe(lambda e: e.matmul(PS[bk][0:64, 0:NW], lhsT=onesf[64:65, 0:64], rhs=rec[64:65, 0:NW], start=True, stop=True),
                         reads=["rec", "onesf"], writes=[("ps", bk)])
                    P.dve(lambda e: e.tensor_tensor(out=attnT[64 * g:64 * g + 64, :, j * QS:(j + 1) * QS],
                                                    in0=tmpO[0:64, 0:NW].rearrange("p (h t) -> p h t", h=4),
                                                    in1=PS[bk][0:64, 0:NW].rearrange("p (h t) -> p h t", h=4), op=ALU.mult),
                          reads=["tmpO", ("ps", bk)], writes=["attnT"])

                nu = len(units)
                for u in range(nu + 2):
                    def s_u(u=u):
                        if u < nu:
                            emit_Q(u)
                        if 2 <= u:
                            emit_V(u - 2)
                        if u < nu:
                            emit_X(u)
                    steps.append(s_u)
                steps.insert(nst + 2, lambda: emit_fin(0))
                steps.append(lambda: emit_fin(1))
                return steps

            def run_interleaved(a, b):
                na, nb = len(a), len(b)
                ia = ib = 0
                while ia < na or ib < nb:
                    if ib >= nb or (ia < na and ia * nb <= ib * na):
                        a[ia]()
                        ia += 1
                    else:
                        b[ib]()
                        ib += 1

            ATT_FRAC_IN_IDX = 0.0
            run_interleaved(idx_steps(0), [])
            prev_att = []
            for j in range(NS):
                nxt = idx_steps(j + 1, overlapped=use_arena) if j + 1 < NS else []
                pe_stream = nxt + list(prev_att)
                k0, kend = None, None
                if use_arena and len(nxt) > 0 and not prev_att and _DBG.get("assist", True):
                    nb_ = NIT + 2
                    k0 = 0
                    kend = NIT if not prev_att else max(0, (len(nxt) * nb_) // max(1, len(pe_stream)) - 2)
                run_interleaved(pe_stream, bis_steps(j, k0, kend=kend))
                do_mask(j)
                prev_att = att_steps(j)
            for _w in range(16):
                P.pe(lambda e: e.matmul(PS[1][:, :], lhsT=kt[:, 0:128], rhs=kt[:, 0:512], start=True, stop=True),
                     reads=[ktn], writes=[("ps", 1)])
            run_interleaved(prev_att, [])

            if _DBG["stop"] <= 3:
                return
            if use_arena:
                wstate["cap"] = None
                P.dma("sp", "x1l", lambda e: e.dma_start(out=arena[:, 0:4096], in_=x1[x1i]),
                      reads=[("x1", x1i)], writes=[("xT", c) for c in range(8)])

            def mixrhs(kc):
                return convT[:, kc, 0:T] if kc < 4 else attnT[:, kc - 4, 0:T]
            for s in range(2):
                wt, wres = wslot()
                for ctl in range(4):
                    ct = s * 4 + ctl
                    b = proj_tile(wt, wres, ctl, T, mixrhs, 8, ["convT", "attnT"])
                    resid_evac(b, ct, T)
            ln_apply(l, 0, T)

            if _DBG["stop"] <= 4:
                return
            for s in range(8):
                wt, wres = wslot()
                for ctl in range(4):
                    ct = s * 4 + ctl
                    b = proj_tile(wt, wres, ctl, T, xrhs, 8, xb_reads)
                    rt = rtmp[ct % 2]
                    P.act(lambda e, b=b, rt=rt: e.activation(out=rt[:, 0:T], in_=PS[b][:, 0:T], func=AF.Relu),
                          reads=[("ps", b)], writes=[("Rh", ct % 2)])
                    P.pool(lambda e, ct=ct, rt=rt: e.tensor_tensor(out=hid[:, ct * 512:ct * 512 + T], in0=rt[:, 0:T], in1=rt[:, 0:T], op=ALU.mult),
                           reads=[("Rh", ct % 2)], writes=["big"])
            for ct in range(8):
                wt, wres = wslot()
                b = proj_tile(wt, wres, 0, T, lambda kc: hid[:, kc * 512:kc * 512 + T], 32, ["big"])
                resid_evac(b, ct, T)
            ln_apply(l, 1, T)

            if _DBG["stop"] <= 5:
                return
            if l < nlayers - 1:
                if samp:
                    P.dma("pool", "x1s", lambda e: e.dma_start(out=x1[x1i].rearrange("p (c t) -> p c t", c=8)[:, :, 0:T], in_=xT[:, :, 0:T]),
                          reads=[("xT", c) for c in range(8)], writes=[("x1", x1i)])
                else:
                    P.dma("pool", "x1s", lambda e: e.dma_start(out=x1[x1i], in_=arena[:, 0:4096]),
                          reads=[("xT", c) for c in range(8)], writes=[("x1", x1i)])
            else:
                dst = ys if samp else y
                for j in range(NS):
                    sg = stg[j % 2]
                    sgn = ("stg", j % 2)
                    for half in range(2):
                        b = rr()
                        for c4 in range(4):
                            c = half * 4 + c4
                            P.pe(lambda e, c=c, c4=c4, b=b, j=j: e.matmul(PS[b][0:QS, c4 * 128:(c4 + 1) * 128], lhsT=xT[:, c, j * QS:(j + 1) * QS], rhs=ident[:, :], start=True, stop=True),
                                 reads=[("xT", c), "consts"], writes=[("ps", b)])
                        evac(half, sg[0:QS, half * 512:half * 512 + 512], PS[b][0:QS, :], [("ps", b)], [sgn])
                    r0 = j * QS if samp else 512 * ti + j * QS
                    P.dma("pool", "so%d" % (j % 2), lambda e, sg=sg, r0=r0, dst=dst: e.dma_start(out=dst[r0:r0 + QS, :], in_=sg[0:QS, :]),
                          reads=[sgn], writes=["y_out"])

        tiles = []
        for l in range(nlayers):
            if do_sample:
                tiles.append((l, "s", 0))
            for ti in range(NT):
                tiles.append((l, "p", ti))
        for (l, kind, ti) in tiles:
            for s in range(24):
                wseq.append((l, s))
        lcast = [(1, s) for s in range(24)] if (nlayers > 1 and not _DBG.get('nol1cast')) else []
        if _DBG["maxtiles"] is not None:
            tiles = tiles[:_DBG["maxtiles"]]
        for n, (l, kind, ti) in enumerate(tiles):
            if (l == 0 and n >= 2) or l > 0 or n == len([t for t in tiles if t[0] == 0]) - 1:
                for _ in range(8 if l == 0 and n < len([t for t in tiles if t[0] == 0]) - 1 else 24):
                    if lcast:
                        emit_cast(*lcast.pop(0))
            process_tile(l, kind, ti)
        P.emit()
    return nc


_NC_CACHE = {}
_DBG = {"stop": 99, "maxtiles": None, "ncast": 24}


def kernel(x_prompt, x_sample, cache_k, cache_v, cache_kidx, state_conv,
           w_in, conv_w, w_o, ln1_g, ln1_b, w_ff1, w_ff2, ln2_g, ln2_b, rel_bias):
    f = lambda a: np.ascontiguousarray(np.asarray(a, dtype=np.float32))
    x_prompt, x_sample = f(x_prompt), f(x_sample)
    B, SEQ = x_prompt.shape[0], x_prompt.shape[1]
    wall = _prep_weights(f(w_in), f(w_o), f(w_ff1), f(w_ff2))
    lnp = np.zeros((128, DEPTH * 32), np.float32)
    for l in range(DEPTH):
        for wi_, arr in enumerate([ln1_g, ln1_b, ln2_g, ln2_b]):
            lnp[:, (l * 4 + wi_) * 8:(l * 4 + wi_) * 8 + 8] = f(arr)[l].reshape(8, 128).T
    cwp = np.zeros((128, DEPTH * 12), np.float32)
    cw = f(conv_w)
    for l in range(DEPTH):
        for j in range(3):
            cwp[:, (l * 3 + j) * 4:(l * 3 + j) * 4 + 4] = cw[l, j].reshape(4, 128).T
    consts = _static_consts()
    ck, cv_, cki, scv = f(cache_k), f(cache_v), f(cache_kidx), f(state_conv)
    if SEQ not in _NC_CACHE:
        _NC_CACHE[SEQ] = build(SEQ, do_sample=not _DBG.get('nosample', False))
    nc = _NC_CACHE[SEQ]
    in_maps = []
    for c in range(B):
        m = {"xp": x_prompt[c], "xs": x_sample[c],
             "ck": np.ascontiguousarray(ck[:, c].reshape(DEPTH, 1024, 128)),
             "cv": np.ascontiguousarray(cv_[:, c].reshape(DEPTH, 1024, 128)),
             "cki": np.ascontiguousarray(cki[:, c]),
             "scv": np.ascontiguousarray(scv[:, c]),
             "wall": wall, "lnp": lnp, "cwp": cwp, "rb": f(rel_bias)}
        m.update(consts)
        in_maps.append(m)
    res = run_bass_kernel_spmd(nc, in_maps, core_ids=list(range(B)))
    R = res.results
    y = np.stack([R[c]["y"] for c in range(B)])
    ys = np.stack([R[c]["ys"] for c in range(B)])
    pk = np.stack([R[c]["pk"].reshape(DEPTH, SEQ, 2, 64) for c in range(B)], axis=1)
    pv = np.stack([R[c]["pv"].reshape(DEPTH, SEQ, 2, 64) for c in range(B)], axis=1)
    pki = np.stack([R[c]["pki"] for c in range(B)], axis=1)
    pc = np.stack([R[c]["pc"] for c in range(B)], axis=1)
    sk = np.stack([R[c]["sk"].reshape(DEPTH, 32, 2, 64) for c in range(B)], axis=1)
    sv = np.stack([R[c]["sv"].reshape(DEPTH, 32, 2, 64) for c in range(B)], axis=1)
    ski = np.stack([R[c]["ski"] for c in range(B)], axis=1)
    sc = np.stack([R[c]["sc"] for c in range(B)], axis=1)
    return (y, ys, pk, pv, pki, pc, sk, sv, ski, sc)
```

```python
import contextlib
import math
import numpy as np
import concourse.bass as bass
import concourse.mybir as mybir
from concourse.bass_utils import run_bass_kernel_spmd

F32 = mybir.dt.float32
BF16 = mybir.dt.bfloat16
ALU = mybir.AluOpType
AF = mybir.ActivationFunctionType
AX = mybir.AxisListType

D = 1024
DEPTH = 2
ALPHA = (2 * DEPTH) ** 0.25
LN_EPS = 1e-5
NIT = 18
NSLOT = 3
MASKV = -30000.0


class Prog:
    ENG = ("pe", "act", "dve", "pool", "sp")

    def __init__(self, nc):
        self.nc = nc
        self.ops = []
        self.last_w = {}
        self.readers = {}
        self.chan_cnt = {}

    def add(self, eng, fn, reads=(), writes=(), chan=None):
        idx = len(self.ops)
        ps_reads = [r for r in reads if isinstance(r, tuple) and r[0] == "ps"]
        if ps_reads:
            reads = [r for r in reads if r not in ps_reads]
            writes = list(writes) + ps_reads
        deps = set()
        for r in reads:
            w = self.last_w.get(r)
            if w is not None:
                deps.add(w)
        for r in writes:
            w = self.last_w.get(r)
            if w is not None:
                deps.add(w)
            for rd in self.readers.get(r, ()):
                deps.add(rd)
        for r in reads:
            self.readers.setdefault(r, []).append(idx)
        for r in writes:
            self.last_w[r] = idx
            self.readers[r] = []
        deps.discard(idx)
        op = dict(eng=eng, fn=fn, deps=deps, chan=chan, need_inc=False)
        if chan is not None:
            self.chan_cnt[chan] = self.chan_cnt.get(chan, 0) + 16
            op["cval"] = self.chan_cnt[chan]
        self.ops.append(op)
        return idx

    def pe(self, fn, reads=(), writes=()):
        return self.add("pe", fn, reads, writes)

    def act(self, fn, reads=(), writes=()):
        return self.add("act", fn, reads, writes)

    def dve(self, fn, reads=(), writes=()):
        return self.add("dve", fn, reads, writes)

    def pool(self, fn, reads=(), writes=()):
        return self.add("pool", fn, reads, writes)

    def dma(self, eng, chan, fn, reads=(), writes=()):
        return self.add(eng, fn, reads, writes, chan=chan)

    def emit(self):
        nc = self.nc
        ops = self.ops
        for op in ops:
            for d in op["deps"]:
                p = ops[d]
                if p["chan"] is None:
                    if p["eng"] == "pe" and op["eng"] == "pe" and op["chan"] is None:
                        continue
                    p["need_inc"] = True
        tick = {e: 0 for e in self.ENG}
        for op in ops:
            if op["chan"] is None and op["need_inc"]:
                tick[op["eng"]] += 1
                op["tick"] = tick[op["eng"]]
        with contextlib.ExitStack() as st:
            esem = {e: st.enter_context(nc.semaphore("s_" + e)) for e in self.ENG}
            csem = {c: st.enter_context(nc.semaphore("c_%s" % (str(c),)))
                    for c in self.chan_cnt}
            block = st.enter_context(nc.Block())
            engobj = {"pe": "tensor", "act": "scalar", "dve": "vector",
                      "pool": "gpsimd", "sp": "sync"}
            per_eng = {e: [] for e in self.ENG}
            for op in ops:
                per_eng[op["eng"]].append(op)
            for e in self.ENG:
                lst = per_eng[e]

                def body(eobj, lst=lst, e=e):
                    seen = {}
                    for op in lst:
                        waits = {}
                        for d in op["deps"]:
                            p = ops[d]
                            if p["chan"] is not None:
                                key = ("c", p["chan"])
                                val = p["cval"]
                            else:
                                if p["eng"] == "pe" and e == "pe" and op["chan"] is None:
                                    continue
                                key = ("e", p["eng"])
                                val = p["tick"]
                            if val > waits.get(key, 0):
                                waits[key] = val
                        for key, val in waits.items():
                            if seen.get(key, 0) >= val:
                                continue
                            seen[key] = val
                            sem = csem[key[1]] if key[0] == "c" else esem[key[1]]
                            eobj.wait_ge(sem, val)
                        ins = op["fn"](eobj)
                        if op["chan"] is not None:
                            ins.then_inc(csem[op["chan"]], 16)
                        elif op["need_inc"]:
                            ins.then_inc(esem[e], 1)
                    if e == "sp":
                        for c, v in self.chan_cnt.items():
                            if seen.get(("c", c), 0) < v:
                                eobj.wait_ge(csem[c], v)

                getattr(block, engobj[e])(body)


def _bucket_np(rel):
    nb = 16
    ret = (rel > 0).astype(np.int32) * nb
    n = np.abs(rel)
    me = 8
    nf = np.maximum(n, 1).astype(np.float32)
    large = me + (np.log(nf / np.float32(me)) / np.float32(math.log(128 / me))
                  * np.float32(nb - me)).astype(np.int32)
    large = np.minimum(large, nb - 1)
    return ret + np.where(n < me, n, large)


def _static_consts():
    c = {}
    c["ident"] = np.eye(128, dtype=np.float32)
    c["antiid"] = np.ascontiguousarray(np.eye(128, dtype=np.float32)[::-1])
    j = np.arange(512)
    rel = 255 - j
    b = _bucket_np(rel.astype(np.int32))
    oh = np.zeros((32, 512), np.float32)
    oh[b, j] = 1.0
    oh[15, :] -= 1.0
    oh[:, 511] = 0.0
    c["ohvec"] = oh
    sel = np.zeros((8, 4, 128), np.float32)
    for jj in range(4):
        sel[2 * jj, jj, 0:64] = 1.0
        sel[2 * jj + 1, jj, 64:128] = 1.0
    c["sel"] = sel.reshape(8, 512)
    k = np.arange(NIT + 2)
    c["pow2"] = np.ascontiguousarray(np.broadcast_to((0.5 ** (k + 1)).astype(np.float32), (128, NIT + 2)))
    return c


_IN_COLS = None


def _in_cols():
    cols = []
    cols += list(range(0, 512))
    cols += list(range(512, 1024))
    cols += list(range(1024, 1536))
    cols += list(range(2880, 2888)) + [-1] * 120
    for jj in range(4):
        cols += list(range(1536 + 64 * jj, 1536 + 64 * jj + 64))
        cols += list(range(1536 + 64 * (4 + jj), 1536 + 64 * (4 + jj) + 64))
    cols += list(range(2048, 2176))
    cols += list(range(2176, 2304))
    cols += list(range(2816, 2880)) * 2
    cols += list(range(2304, 2816))
    return np.array(cols)


def _tile_w(wm, nct, nkc):
    return wm.reshape(nkc, 128, nct, 128).transpose(2, 1, 0, 3)


def _prep_weights(w_in, w_o, w_ff1, w_ff2):
    cols = _in_cols()
    out = np.zeros((DEPTH, 24, 128, 4096), np.float32)
    rows_o = []
    for kc in range(8):
        for p in range(128):
            if kc < 4:
                rows_o.append(128 * kc + p)
            else:
                jj = kc - 4
                rows_o.append(512 + 64 * jj + p if p < 64 else 512 + 64 * (4 + jj) + (p - 64))
    rows_o = np.array(rows_o)
    for l in range(DEPTH):
        wi = np.zeros((1024, 3072), np.float32)
        valid = cols >= 0
        wi[:, valid] = w_in[l][:, cols[valid]]
        t = _tile_w(wi, 24, 8)
        out[l, 0:6] = t.reshape(6, 4, 128, 1024).transpose(0, 2, 1, 3).reshape(6, 128, 4096)
        t = _tile_w(w_o[l][rows_o, :], 8, 8)
        out[l, 6:8] = t.reshape(2, 4, 128, 1024).transpose(0, 2, 1, 3).reshape(2, 128, 4096)
        t = _tile_w(w_ff1[l], 32, 8)
        out[l, 8:16] = t.reshape(8, 4, 128, 1024).transpose(0, 2, 1, 3).reshape(8, 128, 4096)
        t = _tile_w(w_ff2[l], 8, 32)
        out[l, 16:24] = t.reshape(8, 128, 4096)
    return out


def build(SEQ=8192, do_sample=True, nlayers=DEPTH):
    nc = bass.Bass("TRN2", target_bir_lowering=False)
    NT = SEQ // 512
    NST = SEQ // 128

    def din(name, shape, dt=F32):
        return nc.dram_tensor(name, shape, dt, kind="ExternalInput").ap()

    def dout(name, shape, dt=F32):
        return nc.dram_tensor(name, shape, dt, kind="ExternalOutput").ap()

    xp = din("xp", [SEQ, D])
    xs = din("xs", [32, D])
    ck = din("ck", [DEPTH, 1024, 128])
    cv = din("cv", [DEPTH, 1024, 128])
    cki = din("cki", [DEPTH, 1024, 64])
    scv = din("scv", [DEPTH, 2, 512])
    wall = din("wall", [DEPTH, 24, 128, 4096])
    lnp = din("lnp", [128, DEPTH * 4 * 8])
    cwp = din("cwp", [128, DEPTH * 3 * 4])
    rbd = din("rb", [32, 8])
    d_ident = din("ident", [128, 128])
    d_anti = din("antiid", [128, 128])
    d_oh = din("ohvec", [32, 512])
    d_sel = din("sel", [8, 512])
    d_pow2 = din("pow2", [128, NIT + 2])

    y = dout("y", [SEQ, D])
    ys = dout("ys", [32, D])
    pk = dout("pk", [DEPTH, SEQ, 128])
    pv = dout("pv", [DEPTH, SEQ, 128])
    pki = dout("pki", [DEPTH, SEQ, 64])
    pc = dout("pc", [DEPTH, 2, 512])
    sk = dout("sk", [DEPTH, 32, 128])
    sv = dout("sv", [DEPTH, 32, 128])
    ski = dout("ski", [DEPTH, 32, 64])
    sc = dout("sc", [DEPTH, 2, 512])

    wb = nc.dram_tensor("wb", [DEPTH, 24, 128, 4096], BF16).ap()
    x1 = nc.dram_tensor("x1", [NT + 1, 128, 8 * 512], F32).ap()
    fd = nc.dram_tensor("fd", [8, 512], F32).ap()

    st = contextlib.ExitStack()
    with st:
        def sb(name, shape, dt):
            return st.enter_context(nc.sbuf_tensor(name, shape, dt))

        def psum(name, shape, dt):
            return st.enter_context(nc.psum_tensor(name, shape, dt))

        KT = sb("KT", [128, SEQ], BF16)
        VA = sb("VA", [128, NST, 2, 65], BF16)
        KiT = sb("KiT", [128, SEQ], BF16)
        KTs = sb("KTs", [128, 1056], BF16)
        VAs = sb("VAs", [128, 9, 2, 65], BF16)
        KiTs = sb("KiTs", [128, 1056], BF16)
        arena = sb("arena", [128, 8192], F32)
        xT = arena[:, 0:4096].rearrange("p (c t) -> p c t", c=8)
        xb = arena[:, 4096:6144].bitcast(BF16).rearrange("p (c t) -> p c t", c=8)
        big = sb("big", [128, 8192], F32)
        mneg = sb("mneg", [128, 8192], BF16)
        wbuf = [arena[:, 6144:8192].bitcast(BF16)] + [sb("wbuf%d" % k, [128, 4096], BF16)[:, :] for k in range(1, NSLOT)]
        ARENA_RES = [("xT", c) for c in range(8)] + [("xb", c) for c in range(8)] + [("w", 0)]
        qT4 = sb("qT4", [128, 4, 512], BF16)
        qiT = sb("qiT", [128, 4, 512], BF16)
        convT = sb("convT", [128, 4, 512], BF16)
        attnT = sb("attnT", [128, 4, 512], BF16)
        PT = [sb("PT%d" % k, [128, 512], BF16) for k in range(2)]
        uT = sb("uT", [128, 4, 516], F32)
        stg = [sb("stg%d" % k, [128, 1024], F32) for k in range(2)]
        lnm = sb("lnm", [128, 512], F32)
        lnr = sb("lnr", [128, 512], F32)
        Rh = [sb("Rh%d" % k, [128, 512], BF16) for k in range(4)]
        Dh = sb("Dh", [128, 8, 128], BF16)
        Qz = [sb("Qz%d" % g, [128, 4, 128], BF16) for g in range(2)]
        tmpO = sb("tmpO", [128, 512], F32)
        rec = sb("rec", [128, 512], F32)
        sgnT = sb("sgnT", [128, 4, 8], F32)
        biasT = sb("biasT", [128, 2, 2, 4, 128], BF16)
        ident = sb("identf", [128, 128], F32)
        identb = sb("identb", [128, 128], BF16)
        I4 = sb("I4", [128, 4, 128], BF16)
        onesb = sb("onesb", [128, 128], BF16)
        onesf = sb("onesf", [128, 64], F32)
        rbs = sb("rbs", [32, 8], F32)
        pow2 = sb("pow2s", [128, NIT + 2], F32)
        lnps = sb("lnps", [128, DEPTH * 32], F32)
        cws = sb("cws", [128, DEPTH * 12], F32)
        bis = sb("bis", [128, 64], F32)
        PS = [psum("ps%d" % k, [128, 512], F32) for k in range(8)]

        hid = big[:].bitcast(BF16)
        lnn = rec
        antib = Rh[3][:, 0:128]
        lnt = tmpO
        wsT = lnm
        wabsT = lnr
        rtmp = [Rh[0], Rh[1]]
        ohv = lnm
        zb = mneg[:, 0:4096]
        zsq = mneg[:, 4096:8192]

        P = Prog(nc)
        rr_state = [0]

        def rr():
            rr_state[0] = (rr_state[0] + 1) % 8
            return rr_state[0]

        def cload(dst, src):
            P.dma("sp", "cst", lambda e: e.dma_start(out=dst, in_=src), writes=["consts"])
        cload(ident[:], d_ident)
        cload(tmpO[:, 0:128], d_anti)
        cload(lnm[0:32, :], d_oh)
        cload(rbs[:], rbd)
        cload(pow2[:], d_pow2)
        cload(lnps[:], lnp)
        cload(cws[:], cwp)
        P.dve(lambda e: e.tensor_copy(out=identb[:], in_=ident[:]), reads=["consts"], writes=["identb"])
        P.dve(lambda e: e.tensor_copy(out=antib[:], in_=tmpO[:, 0:128]), reads=["consts"], writes=[("Rh", 3), "tmpO"])
        for k in range(4):
            P.dve(lambda e, k=k: e.tensor_copy(out=I4[:, k, :], in_=ident[:]), reads=["consts"], writes=["I4"])
        P.pool(lambda e: e.memset(onesb[:], 1.0 / 1024.0), writes=["onesb"])
        P.pool(lambda e: e.memset(onesf[:], 1.0), writes=["onesf"])
        for g in range(2):
            P.pool(lambda e, g=g: e.memset(Qz[g][:], 0.0), writes=[("Qz", g)])
        P.pool(lambda e: e.memset(VA[:, :, :, 64:65], 1.0), writes=["VA"])
        P.pool(lambda e: e.memset(VAs[:, :, :, 64:65], 1.0), writes=["VAs"])

        P.pe(lambda e: e.matmul(PS[0][0:8, :], lhsT=rbs[:, :], rhs=lnm[0:32, :], start=True, stop=True),
             reads=["consts", "lnm"], writes=[("ps", 0)])
        P.act(lambda e: e.activation(out=rec[0:8, :], in_=PS[0][0:8, :], func=AF.Copy),
              reads=[("ps", 0)], writes=["rec"])
        P.dma("sp", "cst", lambda e: e.dma_start(out=fd, in_=rec[0:8, :]), reads=["rec"], writes=["fd", "consts"])
        for ty in range(2):
            off = 128 + 128 * ty
            src = bass.AP(fd.tensor, off, [[1, 128], [512, 8], [1, 128]])
            P.dma("sp", "cst", lambda e, src=src: e.dma_start(out=big[:, 0:1024].rearrange("p (h t) -> p h t", h=8), in_=src),
                  reads=["fd"], writes=["big", "consts"])
            P.dve(lambda e: e.tensor_copy(out=mneg[:, 0:1024], in_=big[:, 0:1024]), reads=["big"], writes=["mneg"])
            for g in range(2):
                b = 1 + g
                P.pe(lambda e, b=b, g=g: e.matmul(PS[b][:, :], lhsT=antib[:, :], rhs=mneg[:, 512 * g:512 * g + 512],
                                                 start=True, stop=True),
                     reads=["mneg", ("Rh", 3)], writes=[("ps", b)])
                P.act(lambda e, b=b, g=g, ty=ty: e.activation(
                    out=biasT[:, ty, g, :, :].rearrange("p h t -> p (h t)"), in_=PS[b][:, :], func=AF.Copy),
                    reads=[("ps", b)], writes=["biasT"])

        cast_state = {"emitted": set()}

        def emit_cast(l, s):
            if (l, s) in cast_state["emitted"]:
                return
            cast_state["emitted"].add((l, s))
            P.dma("pool", "wc%d_%d" % (l, s), lambda e: e.dma_start(out=wb[l, s], in_=wall[l, s]),
                  writes=[("wbd", l, s)])

        for s in range(_DBG.get("ncast", 24)):
            emit_cast(0, s)

        wseq = []
        wstate = {"emitted": 0}

        def w_emit_upto(n):
            if wstate.get("cap") is not None:
                n = min(n, wstate["cap"])
            while wstate["emitted"] <= min(n, len(wseq) - 1):
                i = wstate["emitted"]
                l, s = wseq[i]
                k = i % NSLOT
                P.dma("sp", "w%d" % k, lambda e, l=l, s=s, k=k: e.dma_start(out=wbuf[k][:, :], in_=wb[l, s]),
                      reads=[("wbd", l, s)], writes=[("w", k)])
                wstate["emitted"] += 1

        wcur = {"n": 0}

        def wslot():
            n = wcur["n"]
            wcur["n"] += 1
            w_emit_upto(n + NSLOT - 1)
            k = n % NSLOT
            return wbuf[k], ("w", k)

        def evac(i, out, in_, reads, writes, scale=None):
            if i % 2 == 0:
                if scale is None:
                    P.act(lambda e: e.activation(out=out, in_=in_, func=AF.Copy), reads=reads, writes=writes)
                else:
                    P.act(lambda e: e.activation(out=out, in_=in_, func=AF.Copy, scale=scale), reads=reads, writes=writes)
            else:
                if scale is None:
                    P.dve(lambda e: e.tensor_copy(out=out, in_=in_), reads=reads, writes=writes)
                else:
                    P.dve(lambda e: e.tensor_scalar(out=out, in0=in_, scalar1=scale, scalar2=None, op0=ALU.mult),
                          reads=reads, writes=writes)

        def proj_tile(wt, wres, ctl, T, rhs_fn, nkc, reads):
            b = rr()
            for kc in range(nkc):
                P.pe(lambda e, kc=kc, b=b: e.matmul(PS[b][:, 0:T], lhsT=wt[:, (ctl * nkc + kc) * 128:(ctl * nkc + kc) * 128 + 128],
                                                   rhs=rhs_fn(kc), start=(kc == 0), stop=(kc == nkc - 1)),
                     reads=[wres] + reads, writes=[("ps", b)])
            return b

        def ln_apply(l, which, T):
            bm, be = rr(), rr()
            for c in range(8):
                P.pe(lambda e, c=c: e.matmul(PS[bm][:, 0:T], lhsT=onesb[:, :], rhs=zb[:, c * 512:c * 512 + T],
                                             start=(c == 0), stop=(c == 7)), reads=["mneg", "onesb"], writes=[("ps", bm)])
            for c in range(8):
                P.pe(lambda e, c=c: e.matmul(PS[be][:, 0:T], lhsT=onesb[:, :], rhs=zsq[:, c * 512:c * 512 + T],
                                             start=(c == 0), stop=(c == 7)), reads=["mneg", "onesb"], writes=[("ps", be)])
            P.act(lambda e: e.activation(out=lnm[:, 0:T], in_=PS[bm][:, 0:T], func=AF.Copy), reads=[("ps", bm)], writes=["lnm"])
            P.pool(lambda e: e.tensor_tensor(out=lnt[:, 0:T], in0=lnm[:, 0:T], in1=lnm[:, 0:T], op=ALU.mult),
                   reads=["lnm"], writes=["tmpO"])
            P.dve(lambda e: e.tensor_tensor(out=lnr[:, 0:T], in0=PS[be][:, 0:T], in1=lnt[:, 0:T], op=ALU.subtract),
                  reads=[("ps", be), "tmpO"], writes=["lnr"])
            P.dve(lambda e: e.tensor_scalar(out=lnr[:, 0:T], in0=lnr[:, 0:T], scalar1=LN_EPS, scalar2=None, op0=ALU.add),
                  reads=["lnr"], writes=["lnr"])
            P.act(lambda e: e.activation(out=lnt[:, 0:T], in_=lnr[:, 0:T], func=AF.Sqrt), reads=["lnr"], writes=["tmpO"])
            P.dve(lambda e: e.reciprocal(out=lnr[:, 0:T], in_=lnt[:, 0:T]), reads=["tmpO"], writes=["lnr"])
            P.dve(lambda e: e.scalar_tensor_tensor(out=lnn[:, 0:T], in0=lnm[:, 0:T], scalar=-1.0, in1=lnr[:, 0:T],
                                                   op0=ALU.mult, op1=ALU.mult), reads=["lnm", "lnr"], writes=["rec"])
            gcol = (l * 4 + 2 * which) * 8
            bcol = (l * 4 + 2 * which + 1) * 8
            for c in range(8):
                P.pool(lambda e, c=c: e.tensor_tensor(out=xT[:, c, 0:T], in0=xT[:, c, 0:T], in1=lnr[:, 0:T], op=ALU.mult),
                       reads=[("xT", c), "lnr"], writes=[("xT", c)])
                P.dve(lambda e, c=c: e.tensor_tensor(out=xT[:, c, 0:T], in0=xT[:, c, 0:T], in1=lnn[:, 0:T], op=ALU.add),
                      reads=[("xT", c), "rec"], writes=[("xT", c)])
                P.dve(lambda e, c=c: e.tensor_scalar(out=xT[:, c, 0:T], in0=xT[:, c, 0:T], scalar1=lnps[:, gcol + c:gcol + c + 1],
                                                     scalar2=lnps[:, bcol + c:bcol + c + 1], op0=ALU.mult, op1=ALU.add),
                      reads=[("xT", c), "consts"], writes=[("xT", c)])
                P.act(lambda e, c=c: e.activation(out=xb[:, c, 0:T], in_=xT[:, c, 0:T], func=AF.Copy),
                      reads=[("xT", c)], writes=[("xb", c)])

        def resid_evac(b, ct, T):
            P.dve(lambda e: e.scalar_tensor_tensor(out=xT[:, ct, 0:T], in0=xT[:, ct, 0:T], scalar=ALPHA, in1=PS[b][:, 0:T],
                                                   op0=ALU.mult, op1=ALU.add),
                  reads=[("xT", ct), ("ps", b)], writes=[("xT", ct)])
            P.act(lambda e: e.activation(out=zb[:, ct * 512:ct * 512 + T], in_=xT[:, ct, 0:T], func=AF.Copy),
                  reads=[("xT", ct)], writes=["mneg"])
            P.act(lambda e: e.activation(out=zsq[:, ct * 512:ct * 512 + T], in_=xT[:, ct, 0:T], func=AF.Square),
                  reads=[("xT", ct)], writes=["mneg"])

        def process_tile(l, kind, ti):
            samp = (kind == "s")
            T = 32 if samp else 512
            QS = 32 if samp else 128
            NS = 1 if samp else 4
            pos0 = 1024 if samp else 512 * ti
            kt, va, kit = (KTs, VAs, KiTs) if samp else (KT, VA, KiT)
            ktn, van, kitn = ("KTs", "VAs", "KiTs") if samp else ("KT", "VA", "KiT")
            x1i = NT if samp else ti

            if l == 0:
                src = xs if samp else xp
                for j in range(NS):
                    sg = stg[j % 2]
                    sgn = ("stg", j % 2)
                    r0 = j * QS if samp else 512 * ti + j * QS
                    P.dma("sp", "stg%d" % (j % 2), lambda e, sg=sg, r0=r0: e.dma_start(out=sg[0:QS, :], in_=src[r0:r0 + QS, :]),
                          writes=[sgn])
                    for half in range(2):
                        b = rr()
                        for c4 in range(4):
                            c = half * 4 + c4
                            P.pe(lambda e, c=c, c4=c4, b=b, sg=sg: e.matmul(PS[b][:, c4 * 128:c4 * 128 + QS], lhsT=sg[0:QS, c * 128:(c + 1) * 128], rhs=ident[0:QS, 0:QS], start=True, stop=True),
                                 reads=[sgn, "consts"], writes=[("ps", b)])
                        o3 = xT[:, half * 4:half * 4 + 4, j * QS:(j + 1) * QS]
                        ob = xb[:, half * 4:half * 4 + 4, j * QS:(j + 1) * QS]
                        i3 = PS[b][:, :].rearrange("p (c t) -> p c t", c=4)[:, :, 0:QS]
                        wr = [("xT", half * 4 + c4) for c4 in range(4)]
                        wrb = [("xb", half * 4 + c4) for c4 in range(4)]
                        if not _DBG.get("s0_noact"):
                            P.act(lambda e, o3=o3, i3=i3: e.activation(out=o3, in_=i3, func=AF.Copy), reads=[("ps", b)], writes=wr)
                        if not _DBG.get("s0_nodve"):
                            P.dve(lambda e, ob=ob, i3=i3: e.tensor_copy(out=ob, in_=i3), reads=[("ps", b)], writes=wrb)
            else:
                if samp:
                    P.dma("sp", "x1l", lambda e: e.dma_start(out=xT[:, :, 0:T], in_=x1[x1i].rearrange("p (c t) -> p c t", c=8)[:, :, 0:T]),
                          reads=[("x1", x1i)], writes=[("xT", c) for c in range(8)])
                else:
                    P.dma("sp", "x1l", lambda e: e.dma_start(out=arena[:, 0:4096], in_=x1[x1i]),
                          reads=[("x1", x1i)], writes=[("xT", c) for c in range(8)])
                for c in range(8):
                    evac(c, xb[:, c, 0:T], xT[:, c, 0:T], [("xT", c)], [("xb", c)])

            if _DBG["stop"] <= -3:
                return
            if samp:
                P.dma("sp", "stg0", lambda e: e.dma_start(out=stg[0][:, :].rearrange("p (n d) -> p n d", n=8),
                                                          in_=ck[l].rearrange("(n p) d -> p n d", p=128)), writes=[("stg", 0)])
                for half in range(2):
                    b = rr()
                    for n4 in range(4):
                        n = half * 4 + n4
                        P.pe(lambda e, n=n, n4=n4, b=b: e.matmul(PS[b][:, n4 * 128:(n4 + 1) * 128], lhsT=stg[0][:, n * 128:(n + 1) * 128], rhs=ident[:, :], start=True, stop=True),
                             reads=[("stg", 0), "consts"], writes=[("ps", b)])
                    evac(half, KTs[:, half * 512:half * 512 + 512], PS[b][:, :], [("ps", b)], ["KTs"])
                if _DBG["stop"] <= -2:
                    return
                P.dma("sp", "stg1", lambda e: e.dma_start(out=stg[1][:, :].rearrange("p (n d) -> p n d", n=8),
                                                          in_=cv[l].rearrange("(n p) d -> p n d", p=128)), writes=[("stg", 1)])
                P.dve(lambda e: e.tensor_copy(out=VAs[:, 0:8, :, 0:64],
                                              in_=stg[1][:, :].rearrange("p (n g d) -> p n g d", n=8, g=2)),
                      reads=[("stg", 1)], writes=["VAs"])
                if _DBG["stop"] <= -1:
                    return
                P.dma("sp", "stg0", lambda e: e.dma_start(out=stg[0][:, 0:512].rearrange("p (n d) -> p n d", n=8),
                                                          in_=cki[l].rearrange("(n p) d -> p n d", p=128)), writes=[("stg", 0)])
                for half in range(2):
                    b = rr()
                    for n4 in range(4):
                        n = half * 4 + n4
                        P.pe(lambda e, n=n, n4=n4, b=b: e.matmul(PS[b][0:64, n4 * 128:(n4 + 1) * 128], lhsT=stg[0][:, n * 64:(n + 1) * 64], rhs=ident[:, :], start=True, stop=True),
                             reads=[("stg", 0), "consts"], writes=[("ps", b)])
                    P.act(lambda e, half=half, b=b: e.activation(out=KiTs[0:64, half * 512:half * 512 + 512], in_=PS[b][0:64, :], func=AF.Copy),
                          reads=[("ps", b)], writes=["KiTs"])
                    P.dve(lambda e, half=half, b=b: e.tensor_copy(out=KiTs[64:128, half * 512:half * 512 + 512], in_=PS[b][0:64, :]),
                          reads=[("ps", b)], writes=["KiTs"])

            if _DBG["stop"] <= 0:
                return
            use_arena = (not samp)
            if use_arena:
                wstate["cap"] = wcur["n"] + 5
                if l == 0:
                    P.dma("pool", "x1s", lambda e: e.dma_start(out=x1[x1i], in_=arena[:, 0:4096]),
                          reads=[("xT", c) for c in range(8)], writes=[("x1", x1i)])
            xb_reads = [("xb", c) for c in range(8)]

            def xrhs(kc):
                return xb[:, kc, 0:T]

            gb = big[:, 0:2048].rearrange("p (c t) -> p c t", c=4)
            gc = big[:, 2048:4096].rearrange("p (c t) -> p c t", c=4)
            hh = big[:, 4096:6144].rearrange("p (c t) -> p c t", c=4)
            for s in range(3):
                wt, wres = wslot()
                for ctl in range(4):
                    ct = s * 4 + ctl
                    b = proj_tile(wt, wres, ctl, T, xrhs, 8, xb_reads)
                    evac(ct, big[:, ct * 512:ct * 512 + T], PS[b][:, 0:T], [("ps", b)], ["big"])
            if samp:
                for c in range(4):
                    P.dma("sp", "x1l", lambda e, c=c: e.dma_start(out=uT[:, c, 0:2], in_=scv[l][:, c * 128:(c + 1) * 128].rearrange("j p -> p j"),
                                                                  allow_slow_non_contiguous=True), writes=["uT"])
            elif ti == 0:
                P.pool(lambda e: e.memset(uT[:, :, 0:2], 0.0), writes=["uT"])
            else:
                P.pool(lambda e: e.tensor_copy(out=uT[:, :, 0:2], in_=uT[:, :, 512:514]), reads=["uT"], writes=["uT"])
            P.pool(lambda e: e.tensor_tensor(out=uT[:, :, 2:2 + T], in0=gc[:, :, 0:T], in1=hh[:, :, 0:T], op=ALU.mult),
                   reads=["big"], writes=["uT"])
            for c in range(4):
                w0 = cws[:, (l * 3 + 0) * 4 + c:(l * 3 + 0) * 4 + c + 1]
                w1 = cws[:, (l * 3 + 1) * 4 + c:(l * 3 + 1) * 4 + c + 1]
                w2 = cws[:, (l * 3 + 2) * 4 + c:(l * 3 + 2) * 4 + c + 1]
                yy = gc[:, c, 0:T]
                P.dve(lambda e, c=c, w0=w0, yy=yy: e.tensor_scalar(out=yy, in0=uT[:, c, 0:T], scalar1=w0, scalar2=None, op0=ALU.mult),
                      reads=["uT", "consts"], writes=["big"])
                P.dve(lambda e, c=c, w1=w1, yy=yy: e.scalar_tensor_tensor(out=yy, in0=uT[:, c, 1:1 + T], scalar=w1, in1=yy,
                                                                         op0=ALU.mult, op1=ALU.add), reads=["uT", "big"], writes=["big"])
                P.dve(lambda e, c=c, w2=w2, yy=yy: e.scalar_tensor_tensor(out=yy, in0=uT[:, c, 2:2 + T], scalar=w2, in1=yy,
                                                                         op0=ALU.mult, op1=ALU.add), reads=["uT", "big"], writes=["big"])
                P.pool(lambda e, c=c, yy=yy: e.tensor_tensor(out=convT[:, c, 0:T], in0=gb[:, c, 0:T], in1=yy, op=ALU.mult),
                       reads=["big"], writes=["convT"])
            if samp or ti == NT - 1:
                dst = sc if samp else pc
                for c in range(4):
                    P.dma("pool", "pcv", lambda e, c=c, dst=dst: e.dma_start(out=dst[l][:, c * 128:(c + 1) * 128].rearrange("j p -> p j"), in_=uT[:, c, T:T + 2],
                                                                    allow_slow_non_contiguous=True), reads=["uT"], writes=["pc_out"])

            if _DBG["stop"] <= 1:
                return
            kvst = big[:, 6144:7680].rearrange("p (n t) -> p n t", n=3)
            wt, wres = wslot()
            b = proj_tile(wt, wres, 0, T, xrhs, 8, xb_reads)
            wsc = 1.0 / (8.0 * math.sqrt(8.0))
            P.act(lambda e, b=b: e.activation(out=wsT[0:8, 0:T], in_=PS[b][0:8, 0:T], func=AF.Copy, scale=wsc),
                  reads=[("ps", b)], writes=["lnm"])
            bt = rr()
            for j in range(NS):
                P.pe(lambda e, j=j: e.matmul(PS[bt][0:QS, j * 8:j * 8 + 8], lhsT=wsT[0:8, j * QS:(j + 1) * QS], rhs=ident[0:8, 0:8], start=True, stop=True),
                     reads=["lnm", "consts"], writes=[("ps", bt)])
            P.act(lambda e: e.activation(out=sgnT[0:QS, 0:NS, :], in_=PS[bt][0:QS, 0:NS * 8].rearrange("p (n h) -> p n h", h=8),
                                         func=AF.Copy), reads=[("ps", bt)], writes=["sgnT"])
            for jj in range(4):
                if jj == 3:
                    wt, wres = wslot()
                ctl = (1 + jj) % 4
                b = proj_tile(wt, wres, ctl, T, xrhs, 8, xb_reads)
                evac(jj, qT4[:, jj, 0:T], PS[b][:, 0:T], [("ps", b)], ["qT4"], scale=0.125)
            b = proj_tile(wt, wres, 1, T, xrhs, 8, xb_reads)
            P.act(lambda e, b=b: e.activation(out=kvst[:, 0, 0:T], in_=PS[b][:, 0:T], func=AF.Copy), reads=[("ps", b)], writes=["big"])
            P.dve(lambda e, b=b: e.tensor_copy(out=kt[:, pos0:pos0 + T], in_=PS[b][:, 0:T]), reads=[("ps", b)], writes=[ktn])
            b = proj_tile(wt, wres, 2, T, xrhs, 8, xb_reads)
            P.act(lambda e, b=b: e.activation(out=kvst[:, 1, 0:T], in_=PS[b][:, 0:T], func=AF.Copy), reads=[("ps", b)], writes=["big"])
            b = proj_tile(wt, wres, 3, T, xrhs, 8, xb_reads)
            P.act(lambda e, b=b: e.activation(out=kvst[:, 2, 0:T], in_=PS[b][:, 0:T], func=AF.Copy), reads=[("ps", b)], writes=["big"])
            P.dve(lambda e, b=b: e.tensor_copy(out=kit[:, pos0:pos0 + T], in_=PS[b][:, 0:T]), reads=[("ps", b)], writes=[kitn])
            for which in range(3):
                wd = 64 if which == 2 else 128
                b = rr()
                for j in range(NS):
                    P.pe(lambda e, j=j, which=which, wd=wd, b=b: e.matmul(PS[b][0:QS, j * 128:j * 128 + wd], lhsT=kvst[0:wd, which, j * QS:(j + 1) * QS], rhs=ident[0:wd, 0:wd], start=True, stop=True),
                         reads=["big", "consts"], writes=[("ps", b)])
                sg = stg[which % 2]
                sgn = ("stg", which % 2)
                src3 = PS[b][0:QS, :].rearrange("p (n d) -> p n d", n=4)[:, 0:NS, 0:wd]
                dst3 = sg[0:QS, 0:512].rearrange("p (n d) -> p n d", n=4)[:, 0:NS, 0:wd]
                P.act(lambda e, src3=src3, dst3=dst3: e.activation(out=dst3, in_=src3, func=AF.Copy), reads=[("ps", b)], writes=[sgn])
                if which == 1:
                    if samp:
                        vdst = VAs[0:32, 8:9, :, 0:64]
                    else:
                        vdst = VA[:, 4 * ti:4 * ti + 4, :, 0:64]
                    vsrc = PS[b][0:QS, :].rearrange("p (n g d) -> p n g d", n=4, g=2)[:, 0:NS, :, :]
                    P.dve(lambda e, vdst=vdst, vsrc=vsrc: e.tensor_copy(out=vdst, in_=vsrc), reads=[("ps", b)], writes=[van])
                if samp:
                    od = [sk, sv, ski][which][l]
                    P.dma("pool", "so%d" % (which % 2), lambda e, od=od, sg=sg, wd=wd: e.dma_start(out=od, in_=sg[0:32, 0:wd]),
                          reads=[sgn], writes=["kv_out"])
                else:
                    od = [pk, pv, pki][which][l, pos0:pos0 + 512, :].rearrange("(n p) d -> p n d", p=128)
                    P.dma("pool", "so%d" % (which % 2), lambda e, od=od, dst3=dst3: e.dma_start(out=od, in_=dst3),
                          reads=[sgn], writes=["kv_out"])
            wt, wres = wslot()
            for jj in range(4):
                b = proj_tile(wt, wres, jj, T, xrhs, 8, xb_reads)
                evac(jj, qiT[:, jj, 0:T], PS[b][:, 0:T], [("ps", b)], ["qiT"])

            if _DBG["stop"] <= 2:
                return
            NW = 4 * QS

            def Lof(j):
                return 1056 if samp else pos0 + QS * (j + 1)

            def scb(j):
                if use_arena and (j % 2 == 1):
                    return arena, ARENA_RES
                return big, ["big"]

            def idx_steps(j, overlapped=False):
                L = Lof(j)
                nblk = (L + 511) // 512
                steps = []
                SC, SCR = scb(j)

                def s_diag():
                    for h in range(8):
                        P.dve(lambda e, h=h: e.tensor_scalar(out=Dh[0:QS, h, 0:QS], in0=identb[0:QS, 0:QS],
                                                             scalar1=sgnT[0:QS, j, h:h + 1], scalar2=None, op0=ALU.mult),
                              reads=["identb", "sgnT"], writes=["Dh"])
                if overlapped:
                    s_diag()
                else:
                    steps.append(s_diag)
                pairs = [(blk, hp) for blk in range(nblk) for hp in range(4)]

                SB = [(0, 1), (2, 3), (5, 6)]
                RB = [(Rh[0], ("Rh", 0), Rh[1], ("Rh", 1)), (Rh[2], ("Rh", 2), Rh[3], ("Rh", 3)),
                      (PT[0], ("PT", 0), PT[1], ("PT", 1))]

                def emit_A(q):
                    blk, hp = pairs[q]
                    c0 = blk * 512
                    wd = min(512, L - c0)
                    for h2 in range(2):
                        h = 2 * hp + h2
                        pr = 64 * h2
                        bk = SB[q % 3][h2]
                        P.pe(lambda e, h=h, pr=pr, bk=bk: e.matmul(
                            PS[bk][0:QS, 0:wd], lhsT=qiT[pr:pr + 64, h // 2, j * QS:(j + 1) * QS],
                            rhs=kit[pr:pr + 64, c0:c0 + wd], start=True, stop=True),
                            reads=["qiT", kitn], writes=[("ps", bk)])

                def emit_B(q):
                    blk, hp = pairs[q]
                    wd = min(512, L - blk * 512)
                    for h2 in range(2):
                        bk = SB[q % 3][h2]
                        rt, rn = RB[q % 3][2 * h2], RB[q % 3][2 * h2 + 1]
                        if h2 == 0 or overlapped:
                            P.act(lambda e, bk=bk, rt=rt: e.activation(out=rt[0:QS, 0:wd], in_=PS[bk][0:QS, 0:wd], func=AF.Relu),
                                  reads=[("ps", bk)], writes=[rn])
                        else:
                            P.dve(lambda e, bk=bk, rt=rt: e.tensor_scalar(out=rt[0:QS, 0:wd], in0=PS[bk][0:QS, 0:wd], scalar1=0.0,
                                                                          scalar2=None, op0=ALU.max),
                                  reads=[("ps", bk)], writes=[rn])

                def emit_C(q):
                    blk, hp = pairs[q]
                    c0 = blk * 512
                    wd = min(512, L - c0)
                    for h2 in range(2):
                        h = 2 * hp + h2
                        rt, rn = RB[q % 3][2 * h2], RB[q % 3][2 * h2 + 1]
                        P.pe(lambda e, h=h, rt=rt: e.matmul(PS[4][0:QS, 0:wd], lhsT=Dh[0:QS, h, 0:QS],
                                                            rhs=rt[0:QS, 0:wd], start=(h == 0), stop=(h == 7)),
                             reads=["Dh", rn], writes=[("ps", 4)])
                    if hp == 3:
                        P.act(lambda e: e.activation(out=SC[0:QS, c0:c0 + wd], in_=PS[4][0:QS, 0:wd], func=AF.Copy),
                              reads=[("ps", 4)], writes=SCR)

                nq = len(pairs)
                for q in range(nq + 2):
                    def s_q(q=q):
                        if q < nq:
                            emit_A(q)
                        if 2 <= q:
                            emit_C(q - 2)
                        if q < nq:
                            emit_B(q)
                    steps.append(s_q)
                return steps

            def bis_steps(j, k0=None, share=0.30, kend=None):
                L = Lof(j)
                steps = []
                SC, SCR = scb(j)
                need_bis = samp or (L > 256)
                THR = bis[0:QS, 60:61]
                if need_bis:
                    MX, MN, RNG, MID, TMP, T1 = (bis[0:QS, k:k + 1] for k in (0, 1, 2, 3, 4, 5))
                    WT = bis[0:QS, 8:8 + NIT + 2]
                    CNT = bis[0:QS, 32:32 + NIT + 1]
                    CNA = bis[0:QS, 51:52]
                    nA = (int(L * share) // 64) * 64
                    if k0 is None or nA < 64 or L - nA < 64:
                        k0 = NIT
                    La = L - nA

                    def stride0(col, n):
                        x = bis[0:QS, col:col + 1]
                        return bass.AP(x.tensor, x.offset, [list(x.ap[0]), [0, n]])
                    jd = stride0(62, L)
                    jda = stride0(62, La)
                    ja = stride0(61, nA)

                    def s_pre():
                        P.dve(lambda e: e.tensor_reduce(out=MX, in_=SC[0:QS, 0:L], axis=AX.X, op=ALU.max, apply_absolute_value=True),
                              reads=SCR, writes=["bis"])
                        if not samp:
                            P.pool(lambda e: e.memset(SC[0:64, L - 64:L], -1e30), reads=["bis"], writes=SCR)
                        P.dve(lambda e: e.tensor_scalar(out=MN, in0=MX, scalar1=-1.0, scalar2=None, op0=ALU.mult), reads=["bis"], writes=["bis"])
                        P.dve(lambda e: e.tensor_scalar(out=RNG, in0=MX, scalar1=2.0, scalar2=1e-6, op0=ALU.mult, op1=ALU.add),
                              reads=["bis"], writes=["bis"])
                        P.dve(lambda e: e.tensor_scalar(out=WT, in0=pow2[0:QS, :], scalar1=RNG, scalar2=None, op0=ALU.mult),
                              reads=["bis", "consts"], writes=["bis"])
                        P.dve(lambda e: e.tensor_tensor(out=MID, in0=MN, in1=WT[:, 0:1], op=ALU.add), reads=["bis"], writes=["b_mid"])
                        P.dve(lambda e: e.memset(CNT, 0.0), reads=["b_cnt"], writes=["b_cnt"])
                    steps.append(s_pre)
                    for k in range(NIT):
                        def s_it(k=k):
                            if k0 <= k < (NIT if kend is None else kend):
                                P.act(lambda e: e.activation(out=ja, in_=SC[0:QS, La:L], func=AF.Sign, bias=MID, scale=-1.0,
                                                             accum_out=CNA), reads=SCR + ["b_mid"], writes=["b_cna"])
                                P.dve(lambda e: e.tensor_scalar(out=jda, in0=SC[0:QS, 0:La], scalar1=MID, scalar2=0.0,
                                                                op0=ALU.is_ge, op1=ALU.add, accum_out=CNT[:, k:k + 1]),
                                      reads=SCR + ["b_mid"], writes=["b_cnt"])
                                P.dve(lambda e: e.scalar_tensor_tensor(out=T1, in0=CNT[:, k:k + 1], scalar=2.0, in1=CNA,
                                                                       op0=ALU.mult, op1=ALU.subtract), reads=["b_cnt", "b_cna"], writes=["b_tmp"])
                                P.dve(lambda e: e.tensor_scalar(out=TMP, in0=T1, scalar1=float(512 - nA), scalar2=0.5,
                                                                op0=ALU.is_ge, op1=ALU.subtract), reads=["b_tmp"], writes=["b_tmp"])
                            else:
                                P.dve(lambda e: e.tensor_scalar(out=jd, in0=SC[0:QS, 0:L], scalar1=MID, scalar2=0.0,
                                                                op0=ALU.is_ge, op1=ALU.add, accum_out=CNT[:, k:k + 1]),
                                      reads=SCR + ["b_mid"], writes=["b_cnt"])
                                P.dve(lambda e: e.tensor_scalar(out=TMP, in0=CNT[:, k:k + 1], scalar1=256.0, scalar2=0.5,
                                                                op0=ALU.is_ge, op1=ALU.subtract), reads=["b_cnt"], writes=["b_tmp"])
                            P.dve(lambda e: e.scalar_tensor_tensor(out=MID, in0=TMP, scalar=WT[:, k:k + 1], in1=MID,
                                                                   op0=ALU.mult, op1=ALU.add), reads=["b_tmp", "bis"], writes=["b_mid"])
                        steps.append(s_it)

                    def s_thr():
                        P.dve(lambda e: e.tensor_tensor(out=THR, in0=MID, in1=WT[:, NIT:NIT + 1], op=ALU.subtract),
                              reads=["b_mid", "bis"], writes=["bis"])
                    steps.append(s_thr)
                else:
                    def s_thr0():
                        P.pool(lambda e: e.memset(SC[0:64, L - 64:L], -1e30), writes=SCR)
                        P.dve(lambda e: e.memset(THR, -1e29), reads=["bis"], writes=["bis"])
                    steps.append(s_thr0)
                return steps

            def do_mask(j):
                for g in range(2):
                    P.pool(lambda e, g=g: e.tensor_copy(out=Qz[g][64 * g:64 * g + 64, :, 0:QS],
                                                        in_=qT4[64 * g:64 * g + 64, :, j * QS:(j + 1) * QS]),
                           reads=["qT4"], writes=[("Qz", g)])
                L = Lof(j)
                SC, SCR = scb(j)
                THR = bis[0:QS, 60:61]
                P.dve(lambda e: e.tensor_scalar(out=mneg[0:QS, 0:L], in0=SC[0:QS, 0:L], scalar1=THR, scalar2=MASKV,
                                                op0=ALU.is_lt, op1=ALU.mult), reads=SCR + ["bis"], writes=["mneg"])

            def att_steps(j):
                L = Lof(j)
                nst = (L + 127) // 128
                steps = []
                units = [(g, s_) for g in range(2) for s_ in range(nst)]

                def ks_of(s_):
                    return min(128, L - 128 * s_)

                TB = [5, 6, 0]
                OB = [7, 3]
                PB = [(PT[0], ("PT", 0)), (PT[1], ("PT", 1)), (Rh[0], ("Rh", 0))]

                def emit_Q(u):
                    g, s_ = units[u]
                    ks = ks_of(s_)
                    bk = TB[u % 3]
                    if samp:
                        ty = 0 if s_ == 8 else (1 if s_ == 7 else None)
                    else:
                        ty = 0 if s_ == nst - 1 else (1 if s_ == nst - 2 else None)
                    P.pe(lambda e: e.matmul(
                        PS[bk][0:ks, 0:NW], lhsT=kt[:, 128 * s_:128 * s_ + ks],
                        rhs=Qz[g][:, :, 0:QS], start=True, stop=False),
                        reads=[ktn, ("Qz", g)], writes=[("ps", bk)])
                    P.pe(lambda e: e.matmul(
                        PS[bk][0:ks, 0:NW], lhsT=mneg[0:QS, 128 * s_:128 * s_ + ks], rhs=I4[0:QS, :, 0:QS],
                        start=False, stop=(ty is None)), reads=["mneg", "I4"], writes=[("ps", bk)])
                    if ty is not None:
                        P.pe(lambda e: e.matmul(
                            PS[bk][0:ks, 0:NW], lhsT=identb[0:ks, 0:ks], rhs=biasT[0:ks, ty, g, :, 0:QS],
                            start=False, stop=True), reads=["identb", "biasT"], writes=[("ps", bk)])

                def emit_X(u):
                    g, s_ = units[u]
                    ks = ks_of(s_)
                    bk = TB[u % 3]
                    pt, ptn = PB[u % 3]
                    P.act(lambda e: e.activation(out=pt[0:ks, 0:NW], in_=PS[bk][0:ks, 0:NW], func=AF.Exp),
                          reads=[("ps", bk)], writes=[ptn])

                def emit_V(u):
                    g, s_ = units[u]
                    ks = ks_of(s_)
                    pt, ptn = PB[u % 3]
                    ob = OB[g]
                    P.pe(lambda e: e.matmul(
                        PS[ob][0:65, 0:NW], lhsT=va[0:ks, s_, g, :], rhs=pt[0:ks, 0:NW], start=(s_ == 0), stop=(s_ == nst - 1)),
                        reads=[van, ptn], writes=[("ps", ob)])

                def emit_fin(g):
                    ob = OB[g]
                    bk = TB[g]
                    P.dve(lambda e: e.reciprocal(out=rec[64:65, 0:NW], in_=PS[ob][64:65, 0:NW]), reads=[("ps", ob)], writes=["rec"])
                    P.act(lambda e: e.activation(out=tmpO[0:64, 0:NW], in_=PS[ob][0:64, 0:NW], func=AF.Copy),
                          reads=[("ps", ob)], writes=["tmpO"])
                    P.pe(lambda e: e.matmul(PS[bk][0:64, 0:NW], lhsT=onesf[64:65, 0:64], rhs=rec[64:65, 0:NW], start=True, stop=True),
                         reads=["rec", "onesf"], writes=[("ps", bk)])
                    P.dve(lambda e: e.tensor_tensor(out=attnT[64 * g:64 * g + 64, :, j * QS:(j + 1) * QS],
                                                    in0=tmpO[0:64, 0:NW].rearrange("p (h t) -> p h t", h=4),
                                                    in1=PS[bk][0:64, 0:NW].rearrange("p (h t) -> p h t", h=4), op=ALU.mult),
                          reads=["tmpO", ("ps", bk)], writes=["attnT"])

                nu = len(units)
                for u in range(nu + 2):
                    def s_u(u=u):
                        if u < nu:
                            emit_Q(u)
                        if 2 <= u:
                            emit_V(u - 2)
                        if u < nu:
                            emit_X(u)
                    steps.append(s_u)
                steps.append(lambda: emit_fin(0))
                steps.append(lambda: emit_fin(1))
                return steps

            def run_interleaved(a, b):
                na, nb = len(a), len(b)
                ia = ib = 0
                while ia < na or ib < nb:
                    if ib >= nb or (ia < na and ia * nb <= ib * na):
                        a[ia]()
                        ia += 1
                    else:
                        b[ib]()
                        ib += 1

            ATT_FRAC_IN_IDX = 0.0
            run_interleaved(idx_steps(0), [])
            prev_att = []
            for j in range(NS):
                nxt = idx_steps(j + 1, overlapped=use_arena) if j + 1 < NS else []
                pe_stream = nxt + list(prev_att)
                k0, kend = None, None
                if use_arena and len(nxt) > 0 and not prev_att and _DBG.get("assist", True):
                    nb_ = NIT + 2
                    k0 = 0
                    kend = NIT if not prev_att else max(0, (len(nxt) * nb_) // max(1, len(pe_stream)) - 2)
                run_interleaved(pe_stream, bis_steps(j, k0, kend=kend))
                do_mask(j)
                prev_att = att_steps(j)
            for _w in range(10):
                P.pe(lambda e: e.matmul(PS[1][:, :], lhsT=kt[:, 0:128], rhs=kt[:, 0:512], start=True, stop=True),
                     reads=[ktn], writes=[("ps", 1)])
            run_interleaved(prev_att, [])

            if _DBG["stop"] <= 3:
                return
            if use_arena:
                wstate["cap"] = None
                P.dma("sp", "x1l", lambda e: e.dma_start(out=arena[:, 0:4096], in_=x1[x1i]),
                      reads=[("x1", x1i)], writes=[("xT", c) for c in range(8)])

            def mixrhs(kc):
                return convT[:, kc, 0:T] if kc < 4 else attnT[:, kc - 4, 0:T]
            for s in range(2):
                wt, wres = wslot()
                for ctl in range(4):
                    ct = s * 4 + ctl
                    b = proj_tile(wt, wres, ctl, T, mixrhs, 8, ["convT", "attnT"])
                    resid_evac(b, ct, T)
            ln_apply(l, 0, T)

            if _DBG["stop"] <= 4:
                return
            for s in range(8):
                wt, wres = wslot()
                for ctl in range(4):
                    ct = s * 4 + ctl
                    b = proj_tile(wt, wres, ctl, T, xrhs, 8, xb_reads)
                    rt = rtmp[ct % 2]
                    P.act(lambda e, b=b, rt=rt: e.activation(out=rt[:, 0:T], in_=PS[b][:, 0:T], func=AF.Relu),
                          reads=[("ps", b)], writes=[("Rh", ct % 2)])
                    P.pool(lambda e, ct=ct, rt=rt: e.tensor_tensor(out=hid[:, ct * 512:ct * 512 + T], in0=rt[:, 0:T], in1=rt[:, 0:T], op=ALU.mult),
                           reads=[("Rh", ct % 2)], writes=["big"])
            for ct in range(8):
                wt, wres = wslot()
                b = proj_tile(wt, wres, 0, T, lambda kc: hid[:, kc * 512:kc * 512 + T], 32, ["big"])
                resid_evac(b, ct, T)
            ln_apply(l, 1, T)

            if _DBG["stop"] <= 5:
                return
            if l < nlayers - 1:
                if samp:
                    P.dma("pool", "x1s", lambda e: e.dma_start(out=x1[x1i].rearrange("p (c t) -> p c t", c=8)[:, :, 0:T], in_=xT[:, :, 0:T]),
                          reads=[("xT", c) for c in range(8)], writes=[("x1", x1i)])
                else:
                    P.dma("pool", "x1s", lambda e: e.dma_start(out=x1[x1i], in_=arena[:, 0:4096]),
                          reads=[("xT", c) for c in range(8)], writes=[("x1", x1i)])
            else:
                dst = ys if samp else y
                for j in range(NS):
                    sg = stg[j % 2]
                    sgn = ("stg", j % 2)
                    for half in range(2):
                        b = rr()
                        for c4 in range(4):
                            c = half * 4 + c4
                            P.pe(lambda e, c=c, c4=c4, b=b, j=j: e.matmul(PS[b][0:QS, c4 * 128:(c4 + 1) * 128], lhsT=xT[:, c, j * QS:(j + 1) * QS], rhs=ident[:, :], start=True, stop=True),
                                 reads=[("xT", c), "consts"], writes=[("ps", b)])
                        evac(half, sg[0:QS, half * 512:half * 512 + 512], PS[b][0:QS, :], [("ps", b)], [sgn])
                    r0 = j * QS if samp else 512 * ti + j * QS
                    P.dma("pool", "so%d" % (j % 2), lambda e, sg=sg, r0=r0, dst=dst: e.dma_start(out=dst[r0:r0 + QS, :], in_=sg[0:QS, :]),
                          reads=[sgn], writes=["y_out"])

        tiles = []
        for l in range(nlayers):
            if do_sample:
                tiles.append((l, "s", 0))
            for ti in range(NT):
                tiles.append((l, "p", ti))
        for (l, kind, ti) in tiles:
            for s in range(24):
                wseq.append((l, s))
        lcast = [(1, s) for s in range(24)] if (nlayers > 1 and not _DBG.get('nol1cast')) else []
        if _DBG["maxtiles"] is not None:
            tiles = tiles[:_DBG["maxtiles"]]
        for n, (l, kind, ti) in enumerate(tiles):
            if (l == 0 and n >= 2) or l > 0 or n == len([t for t in tiles if t[0] == 0]) - 1:
                for _ in range(8 if l == 0 and n < len([t for t in tiles if t[0] == 0]) - 1 else 24):
                    if lcast:
                        emit_cast(*lcast.pop(0))
            process_tile(l, kind, ti)
        P.emit()
    return nc


_NC_CACHE = {}
_DBG = {"stop": 99, "maxtiles": None, "ncast": 24}


def kernel(x_prompt, x_sample, cache_k, cache_v, cache_kidx, state_conv,
           w_in, conv_w, w_o, ln1_g, ln1_b, w_ff1, w_ff2, ln2_g, ln2_b, rel_bias):
    f = lambda a: np.ascontiguousarray(np.asarray(a, dtype=np.float32))
    x_prompt, x_sample = f(x_prompt), f(x_sample)
    B, SEQ = x_prompt.shape[0], x_prompt.shape[1]
    wall = _prep_weights(f(w_in), f(w_o), f(w_ff1), f(w_ff2))
    lnp = np.zeros((128, DEPTH * 32), np.float32)
    for l in range(DEPTH):
        for wi_, arr in enumerate([ln1_g, ln1_b, ln2_g, ln2_b]):
            lnp[:, (l * 4 + wi_) * 8:(l * 4 + wi_) * 8 + 8] = f(arr)[l].reshape(8, 128).T
    cwp = np.zeros((128, DEPTH * 12), np.float32)
    cw = f(conv_w)
    for l in range(DEPTH):
        for j in range(3):
            cwp[:, (l * 3 + j) * 4:(l * 3 + j) * 4 + 4] = cw[l, j].reshape(4, 128).T
    consts = _static_consts()
    ck, cv_, cki, scv = f(cache_k), f(cache_v), f(cache_kidx), f(state_conv)
    if SEQ not in _NC_CACHE:
        _NC_CACHE[SEQ] = build(SEQ, do_sample=not _DBG.get('nosample', False))
    nc = _NC_CACHE[SEQ]
    in_maps = []
    for c in range(B):
        m = {"xp": x_prompt[c], "xs": x_sample[c],
             "ck": np.ascontiguousarray(ck[:, c].reshape(DEPTH, 1024, 128)),
             "cv": np.ascontiguousarray(cv_[:, c].reshape(DEPTH, 1024, 128)),
             "cki": np.ascontiguousarray(cki[:, c]),
             "scv": np.ascontiguousarray(scv[:, c]),
             "wall": wall, "lnp": lnp, "cwp": cwp, "rb": f(rel_bias)}
        m.update(consts)
        in_maps.append(m)
    res = run_bass_kernel_spmd(nc, in_maps, core_ids=list(range(B)))
    R = res.results
    y = np.stack([R[c]["y"] for c in range(B)])
    ys = np.stack([R[c]["ys"] for c in range(B)])
    pk = np.stack([R[c]["pk"].reshape(DEPTH, SEQ, 2, 64) for c in range(B)], axis=1)
    pv = np.stack([R[c]["pv"].reshape(DEPTH, SEQ, 2, 64) for c in range(B)], axis=1)
    pki = np.stack([R[c]["pki"] for c in range(B)], axis=1)
    pc = np.stack([R[c]["pc"] for c in range(B)], axis=1)
    sk = np.stack([R[c]["sk"].reshape(DEPTH, 32, 2, 64) for c in range(B)], axis=1)
    sv = np.stack([R[c]["sv"].reshape(DEPTH, 32, 2, 64) for c in range(B)], axis=1)
    ski = np.stack([R[c]["ski"] for c in range(B)], axis=1)
    sc = np.stack([R[c]["sc"] for c in range(B)], axis=1)
    return (y, ys, pk, pv, pki, pc, sk, sv, ski, sc)
```

```python
import contextlib
import math
import numpy as np
import concourse.bass as bass
import concourse.mybir as mybir
from concourse.bass_utils import run_bass_kernel_spmd

F32 = mybir.dt.float32
BF16 = mybir.dt.bfloat16
ALU = mybir.AluOpType
AF = mybir.ActivationFunctionType
AX = mybir.AxisListType

D = 1024
DEPTH = 2
ALPHA = (2 * DEPTH) ** 0.25
LN_EPS = 1e-5
NIT = 18
NSLOT = 3
MASKV = -30000.0


class Prog:
    ENG = ("pe", "act", "dve", "pool", "sp")

    def __init__(self, nc):
        self.nc = nc
        self.ops = []
        self.last_w = {}
        self.readers = {}
        self.chan_cnt = {}

    def add(self, eng, fn, reads=(), writes=(), chan=None):
        idx = len(self.ops)
        ps_reads = [r for r in reads if isinstance(r, tuple) and r[0] == "ps"]
        if ps_reads:
            reads = [r for r in reads if r not in ps_reads]
            writes = list(writes) + ps_reads
        deps = set()
        for r in reads:
            w = self.last_w.get(r)
            if w is not None:
                deps.add(w)
        for r in writes:
            w = self.last_w.get(r)
            if w is not None:
                deps.add(w)
            for rd in self.readers.get(r, ()):
                deps.add(rd)
        for r in reads:
            self.readers.setdefault(r, []).append(idx)
        for r in writes:
            self.last_w[r] = idx
            self.readers[r] = []
        deps.discard(idx)
        op = dict(eng=eng, fn=fn, deps=deps, chan=chan, need_inc=False)
        if chan is not None:
            self.chan_cnt[chan] = self.chan_cnt.get(chan, 0) + 16
            op["cval"] = self.chan_cnt[chan]
        self.ops.append(op)
        return idx

    def pe(self, fn, reads=(), writes=()):
        return self.add("pe", fn, reads, writes)

    def act(self, fn, reads=(), writes=()):
        return self.add("act", fn, reads, writes)

    def dve(self, fn, reads=(), writes=()):
        return self.add("dve", fn, reads, writes)

    def pool(self, fn, reads=(), writes=()):
        return self.add("pool", fn, reads, writes)

    def dma(self, eng, chan, fn, reads=(), writes=()):
        return self.add(eng, fn, reads, writes, chan=chan)

    def emit(self):
        nc = self.nc
        ops = self.ops
        for op in ops:
            for d in op["deps"]:
                p = ops[d]
                if p["chan"] is None:
                    if p["eng"] == "pe" and op["eng"] == "pe" and op["chan"] is None:
                        continue
                    p["need_inc"] = True
        tick = {e: 0 for e in self.ENG}
        for op in ops:
            if op["chan"] is None and op["need_inc"]:
                tick[op["eng"]] += 1
                op["tick"] = tick[op["eng"]]
        with contextlib.ExitStack() as st:
            esem = {e: st.enter_context(nc.semaphore("s_" + e)) for e in self.ENG}
            csem = {c: st.enter_context(nc.semaphore("c_%s" % (str(c),)))
                    for c in self.chan_cnt}
            block = st.enter_context(nc.Block())
            engobj = {"pe": "tensor", "act": "scalar", "dve": "vector",
                      "pool": "gpsimd", "sp": "sync"}
            per_eng = {e: [] for e in self.ENG}
            for op in ops:
                per_eng[op["eng"]].append(op)
            for e in self.ENG:
                lst = per_eng[e]

                def body(eobj, lst=lst, e=e):
                    seen = {}
                    for op in lst:
                        waits = {}
                        for d in op["deps"]:
                            p = ops[d]
                            if p["chan"] is not None:
                                key = ("c", p["chan"])
                                val = p["cval"]
                            else:
                                if p["eng"] == "pe" and e == "pe" and op["chan"] is None:
                                    continue
                                key = ("e", p["eng"])
                                val = p["tick"]
                            if val > waits.get(key, 0):
                                waits[key] = val
                        for key, val in waits.items():
                            if seen.get(key, 0) >= val:
                                continue
                            seen[key] = val
                            sem = csem[key[1]] if key[0] == "c" else esem[key[1]]
                            eobj.wait_ge(sem, val)
                        ins = op["fn"](eobj)
                        if op["chan"] is not None:
                            ins.then_inc(csem[op["chan"]], 16)
                        elif op["need_inc"]:
                            ins.then_inc(esem[e], 1)
                    if e == "sp":
                        for c, v in self.chan_cnt.items():
                            if seen.get(("c", c), 0) < v:
                                eobj.wait_ge(csem[c], v)

                getattr(block, engobj[e])(body)


def _bucket_np(rel):
    nb = 16
    ret = (rel > 0).astype(np.int32) * nb
    n = np.abs(rel)
    me = 8
    nf = np.maximum(n, 1).astype(np.float32)
    large = me + (np.log(nf / np.float32(me)) / np.float32(math.log(128 / me))
                  * np.float32(nb - me)).astype(np.int32)
    large = np.minimum(large, nb - 1)
    return ret + np.where(n < me, n, large)


def _static_consts():
    c = {}
    c["ident"] = np.eye(128, dtype=np.float32)
    c["antiid"] = np.ascontiguousarray(np.eye(128, dtype=np.float32)[::-1])
    j = np.arange(512)
    rel = 255 - j
    b = _bucket_np(rel.astype(np.int32))
    oh = np.zeros((32, 512), np.float32)
    oh[b, j] = 1.0
    oh[15, :] -= 1.0
    oh[:, 511] = 0.0
    c["ohvec"] = oh
    sel = np.zeros((8, 4, 128), np.float32)
    for jj in range(4):
        sel[2 * jj, jj, 0:64] = 1.0
        sel[2 * jj + 1, jj, 64:128] = 1.0
    c["sel"] = sel.reshape(8, 512)
    k = np.arange(NIT + 2)
    c["pow2"] = np.ascontiguousarray(np.broadcast_to((0.5 ** (k + 1)).astype(np.float32), (128, NIT + 2)))
    return c


_IN_COLS = None


def _in_cols():
    cols = []
    cols += list(range(0, 512))
    cols += list(range(512, 1024))
    cols += list(range(1024, 1536))
    cols += list(range(2880, 2888)) + [-1] * 120
    for jj in range(4):
        cols += list(range(1536 + 64 * jj, 1536 + 64 * jj + 64))
        cols += list(range(1536 + 64 * (4 + jj), 1536 + 64 * (4 + jj) + 64))
    cols += list(range(2048, 2176))
    cols += list(range(2176, 2304))
    cols += list(range(2816, 2880)) * 2
    cols += list(range(2304, 2816))
    return np.array(cols)


def _tile_w(wm, nct, nkc):
    return wm.reshape(nkc, 128, nct, 128).transpose(2, 1, 0, 3)


def _prep_weights(w_in, w_o, w_ff1, w_ff2):
    cols = _in_cols()
    out = np.zeros((DEPTH, 24, 128, 4096), np.float32)
    rows_o = []
    for kc in range(8):
        for p in range(128):
            if kc < 4:
                rows_o.append(128 * kc + p)
            else:
                jj = kc - 4
                rows_o.append(512 + 64 * jj + p if p < 64 else 512 + 64 * (4 + jj) + (p - 64))
    rows_o = np.array(rows_o)
    for l in range(DEPTH):
        wi = np.zeros((1024, 3072), np.float32)
        valid = cols >= 0
        wi[:, valid] = w_in[l][:, cols[valid]]
        t = _tile_w(wi, 24, 8)
        out[l, 0:6] = t.reshape(6, 4, 128, 1024).transpose(0, 2, 1, 3).reshape(6, 128, 4096)
        t = _tile_w(w_o[l][rows_o, :], 8, 8)
        out[l, 6:8] = t.reshape(2, 4, 128, 1024).transpose(0, 2, 1, 3).reshape(2, 128, 4096)
        t = _tile_w(w_ff1[l], 32, 8)
        out[l, 8:16] = t.reshape(8, 4, 128, 1024).transpose(0, 2, 1, 3).reshape(8, 128, 4096)
        t = _tile_w(w_ff2[l], 8, 32)
        out[l, 16:24] = t.reshape(8, 128, 4096)
    return out


def build(SEQ=8192, do_sample=True, nlayers=DEPTH):
    nc = bass.Bass("TRN2", target_bir_lowering=False)
    NT = SEQ // 512
    NST = SEQ // 128

    def din(name, shape, dt=F32):
        return nc.dram_tensor(name, shape, dt, kind="ExternalInput").ap()

    def dout(name, shape, dt=F32):
        return nc.dram_tensor(name, shape, dt, kind="ExternalOutput").ap()

    xp = din("xp", [SEQ, D])
    xs = din("xs", [32, D])
    ck = din("ck", [DEPTH, 1024, 128])
    cv = din("cv", [DEPTH, 1024, 128])
    cki = din("cki", [DEPTH, 1024, 64])
    scv = din("scv", [DEPTH, 2, 512])
    wall = din("wall", [DEPTH, 24, 128, 4096])
    lnp = din("lnp", [128, DEPTH * 4 * 8])
    cwp = din("cwp", [128, DEPTH * 3 * 4])
    rbd = din("rb", [32, 8])
    d_ident = din("ident", [128, 128])
    d_anti = din("antiid", [128, 128])
    d_oh = din("ohvec", [32, 512])
    d_sel = din("sel", [8, 512])
    d_pow2 = din("pow2", [128, NIT + 2])

    y = dout("y", [SEQ, D])
    ys = dout("ys", [32, D])
    pk = dout("pk", [DEPTH, SEQ, 128])
    pv = dout("pv", [DEPTH, SEQ, 128])
    pki = dout("pki", [DEPTH, SEQ, 64])
    pc = dout("pc", [DEPTH, 2, 512])
    sk = dout("sk", [DEPTH, 32, 128])
    sv = dout("sv", [DEPTH, 32, 128])
    ski = dout("ski", [DEPTH, 32, 64])
    sc = dout("sc", [DEPTH, 2, 512])

    wb = nc.dram_tensor("wb", [DEPTH, 24, 128, 4096], BF16).ap()
    x1 = nc.dram_tensor("x1", [NT + 1, 128, 8 * 512], F32).ap()
    fd = nc.dram_tensor("fd", [8, 512], F32).ap()

    st = contextlib.ExitStack()
    with st:
        def sb(name, shape, dt):
            return st.enter_context(nc.sbuf_tensor(name, shape, dt))

        def psum(name, shape, dt):
            return st.enter_context(nc.psum_tensor(name, shape, dt))

        KT = sb("KT", [128, SEQ], BF16)
        VA = sb("VA", [128, NST, 2, 65], BF16)
        KiT = sb("KiT", [128, SEQ], BF16)
        KTs = sb("KTs", [128, 1056], BF16)
        VAs = sb("VAs", [128, 9, 2, 65], BF16)
        KiTs = sb("KiTs", [128, 1056], BF16)
        arena = sb("arena", [128, 8192], F32)
        xT = arena[:, 0:4096].rearrange("p (c t) -> p c t", c=8)
        xb = arena[:, 4096:6144].bitcast(BF16).rearrange("p (c t) -> p c t", c=8)
        big = sb("big", [128, 8192], F32)
        mneg = sb("mneg", [128, 8192], BF16)
        wbuf = [arena[:, 6144:8192].bitcast(BF16)] + [sb("wbuf%d" % k, [128, 4096], BF16)[:, :] for k in range(1, NSLOT)]
        ARENA_RES = [("xT", c) for c in range(8)] + [("xb", c) for c in range(8)] + [("w", 0)]
        qT4 = sb("qT4", [128, 4, 512], BF16)
        qiT = sb("qiT", [128, 4, 512], BF16)
        convT = sb("convT", [128, 4, 512], BF16)
        attnT = sb("attnT", [128, 4, 512], BF16)
        PT = [sb("PT%d" % k, [128, 512], BF16) for k in range(2)]
        uT = sb("uT", [128, 4, 516], F32)
        stg = [sb("stg%d" % k, [128, 1024], F32) for k in range(2)]
        lnm = sb("lnm", [128, 512], F32)
        lnr = sb("lnr", [128, 512], F32)
        Rh = [sb("Rh%d" % k, [128, 512], BF16) for k in range(4)]
        Dh = sb("Dh", [128, 8, 128], BF16)
        Qz = [sb("Qz%d" % g, [128, 4, 128], BF16) for g in range(2)]
        tmpO = sb("tmpO", [128, 512], F32)
        rec = sb("rec", [128, 512], F32)
        sgnT = sb("sgnT", [128, 4, 8], F32)
        biasT = sb("biasT", [128, 2, 2, 4, 128], BF16)
        ident = sb("identf", [128, 128], F32)
        identb = sb("identb", [128, 128], BF16)
        I4 = sb("I4", [128, 4, 128], BF16)
        onesb = sb("onesb", [128, 128], BF16)
        onesf = sb("onesf", [128, 64], F32)
        rbs = sb("rbs", [32, 8], F32)
        pow2 = sb("pow2s", [128, NIT + 2], F32)
        lnps = sb("lnps", [128, DEPTH * 32], F32)
        cws = sb("cws", [128, DEPTH * 12], F32)
        bis = sb("bis", [128, 64], F32)
        PS = [psum("ps%d" % k, [128, 512], F32) for k in range(8)]

        hid = big[:].bitcast(BF16)
        lnn = rec
        antib = Rh[3][:, 0:128]
        lnt = tmpO
        wsT = lnm
        wabsT = lnr
        rtmp = [Rh[0], Rh[1]]
        ohv = lnm
        zb = mneg[:, 0:4096]
        zsq = mneg[:, 4096:8192]

        P = Prog(nc)
        rr_state = [0]

        def rr():
            rr_state[0] = (rr_state[0] + 1) % 8
            return rr_state[0]

        def cload(dst, src):
            P.dma("sp", "cst", lambda e: e.dma_start(out=dst, in_=src), writes=["consts"])
        cload(ident[:], d_ident)
        cload(tmpO[:, 0:128], d_anti)
        cload(lnm[0:32, :], d_oh)
        cload(rbs[:], rbd)
        cload(pow2[:], d_pow2)
        cload(lnps[:], lnp)
        cload(cws[:], cwp)
        P.dve(lambda e: e.tensor_copy(out=identb[:], in_=ident[:]), reads=["consts"], writes=["identb"])
        P.dve(lambda e: e.tensor_copy(out=antib[:], in_=tmpO[:, 0:128]), reads=["consts"], writes=[("Rh", 3), "tmpO"])
        for k in range(4):
            P.dve(lambda e, k=k: e.tensor_copy(out=I4[:, k, :], in_=ident[:]), reads=["consts"], writes=["I4"])
        P.pool(lambda e: e.memset(onesb[:], 1.0 / 1024.0), writes=["onesb"])
        P.pool(lambda e: e.memset(onesf[:], 1.0), writes=["onesf"])
        for g in range(2):
            P.pool(lambda e, g=g: e.memset(Qz[g][:], 0.0), writes=[("Qz", g)])
        P.pool(lambda e: e.memset(VA[:, :, :, 64:65], 1.0), writes=["VA"])
        P.pool(lambda e: e.memset(VAs[:, :, :, 64:65], 1.0), writes=["VAs"])

        P.pe(lambda e: e.matmul(PS[0][0:8, :], lhsT=rbs[:, :], rhs=lnm[0:32, :], start=True, stop=True),
             reads=["consts", "lnm"], writes=[("ps", 0)])
        P.act(lambda e: e.activation(out=rec[0:8, :], in_=PS[0][0:8, :], func=AF.Copy),
              reads=[("ps", 0)], writes=["rec"])
        P.dma("sp", "cst", lambda e: e.dma_start(out=fd, in_=rec[0:8, :]), reads=["rec"], writes=["fd", "consts"])
        for ty in range(2):
            off = 128 + 128 * ty
            src = bass.AP(fd.tensor, off, [[1, 128], [512, 8], [1, 128]])
            P.dma("sp", "cst", lambda e, src=src: e.dma_start(out=big[:, 0:1024].rearrange("p (h t) -> p h t", h=8), in_=src),
                  reads=["fd"], writes=["big", "consts"])
            P.dve(lambda e: e.tensor_copy(out=mneg[:, 0:1024], in_=big[:, 0:1024]), reads=["big"], writes=["mneg"])
            for g in range(2):
                b = 1 + g
                P.pe(lambda e, b=b, g=g: e.matmul(PS[b][:, :], lhsT=antib[:, :], rhs=mneg[:, 512 * g:512 * g + 512],
                                                 start=True, stop=True),
                     reads=["mneg", ("Rh", 3)], writes=[("ps", b)])
                P.act(lambda e, b=b, g=g, ty=ty: e.activation(
                    out=biasT[:, ty, g, :, :].rearrange("p h t -> p (h t)"), in_=PS[b][:, :], func=AF.Copy),
                    reads=[("ps", b)], writes=["biasT"])

        cast_state = {"emitted": set()}

        def emit_cast(l, s):
            if (l, s) in cast_state["emitted"]:
                return
            cast_state["emitted"].add((l, s))
            P.dma("pool", "wc%d_%d" % (l, s), lambda e: e.dma_start(out=wb[l, s], in_=wall[l, s]),
                  writes=[("wbd", l, s)])

        for s in range(_DBG.get("ncast", 24)):
            emit_cast(0, s)

        wseq = []
        wstate = {"emitted": 0}

        def w_emit_upto(n):
            if wstate.get("cap") is not None:
                n = min(n, wstate["cap"])
            while wstate["emitted"] <= min(n, len(wseq) - 1):
                i = wstate["emitted"]
                l, s = wseq[i]
                k = i % NSLOT
                P.dma("sp", "w%d" % k, lambda e, l=l, s=s, k=k: e.dma_start(out=wbuf[k][:, :], in_=wb[l, s]),
                      reads=[("wbd", l, s)], writes=[("w", k)])
                wstate["emitted"] += 1

        wcur = {"n": 0}

        def wslot():
            n = wcur["n"]
            wcur["n"] += 1
            w_emit_upto(n + NSLOT - 1)
            k = n % NSLOT
            return wbuf[k], ("w", k)

        def evac(i, out, in_, reads, writes, scale=None):
            if i % 2 == 0:
                if scale is None:
                    P.act(lambda e: e.activation(out=out, in_=in_, func=AF.Copy), reads=reads, writes=writes)
                else:
                    P.act(lambda e: e.activation(out=out, in_=in_, func=AF.Copy, scale=scale), reads=reads, writes=writes)
            else:
                if scale is None:
                    P.dve(lambda e: e.tensor_copy(out=out, in_=in_), reads=reads, writes=writes)
                else:
                    P.dve(lambda e: e.tensor_scalar(out=out, in0=in_, scalar1=scale, scalar2=None, op0=ALU.mult),
                          reads=reads, writes=writes)

        def proj_tile(wt, wres, ctl, T, rhs_fn, nkc, reads):
            b = rr()
            for kc in range(nkc):
                P.pe(lambda e, kc=kc, b=b: e.matmul(PS[b][:, 0:T], lhsT=wt[:, (ctl * nkc + kc) * 128:(ctl * nkc + kc) * 128 + 128],
                                                   rhs=rhs_fn(kc), start=(kc == 0), stop=(kc == nkc - 1)),
                     reads=[wres] + reads, writes=[("ps", b)])
            return b

        def ln_apply(l, which, T):
            bm, be = rr(), rr()
            for c in range(8):
                P.pe(lambda e, c=c: e.matmul(PS[bm][:, 0:T], lhsT=onesb[:, :], rhs=zb[:, c * 512:c * 512 + T],
                                             start=(c == 0), stop=(c == 7)), reads=["mneg", "onesb"], writes=[("ps", bm)])
            for c in range(8):
                P.pe(lambda e, c=c: e.matmul(PS[be][:, 0:T], lhsT=onesb[:, :], rhs=zsq[:, c * 512:c * 512 + T],
                                             start=(c == 0), stop=(c == 7)), reads=["mneg", "onesb"], writes=[("ps", be)])
            P.act(lambda e: e.activation(out=lnm[:, 0:T], in_=PS[bm][:, 0:T], func=AF.Copy), reads=[("ps", bm)], writes=["lnm"])
            P.pool(lambda e: e.tensor_tensor(out=lnt[:, 0:T], in0=lnm[:, 0:T], in1=lnm[:, 0:T], op=ALU.mult),
                   reads=["lnm"], writes=["tmpO"])
            P.dve(lambda e: e.tensor_tensor(out=lnr[:, 0:T], in0=PS[be][:, 0:T], in1=lnt[:, 0:T], op=ALU.subtract),
                  reads=[("ps", be), "tmpO"], writes=["lnr"])
            P.dve(lambda e: e.tensor_scalar(out=lnr[:, 0:T], in0=lnr[:, 0:T], scalar1=LN_EPS, scalar2=None, op0=ALU.add),
                  reads=["lnr"], writes=["lnr"])
            P.act(lambda e: e.activation(out=lnt[:, 0:T], in_=lnr[:, 0:T], func=AF.Sqrt), reads=["lnr"], writes=["tmpO"])
            P.dve(lambda e: e.reciprocal(out=lnr[:, 0:T], in_=lnt[:, 0:T]), reads=["tmpO"], writes=["lnr"])
            P.dve(lambda e: e.scalar_tensor_tensor(out=lnn[:, 0:T], in0=lnm[:, 0:T], scalar=-1.0, in1=lnr[:, 0:T],
                                                   op0=ALU.mult, op1=ALU.mult), reads=["lnm", "lnr"], writes=["rec"])
            gcol = (l * 4 + 2 * which) * 8
            bcol = (l * 4 + 2 * which + 1) * 8
            for c in range(8):
                P.pool(lambda e, c=c: e.tensor_tensor(out=xT[:, c, 0:T], in0=xT[:, c, 0:T], in1=lnr[:, 0:T], op=ALU.mult),
                       reads=[("xT", c), "lnr"], writes=[("xT", c)])
                P.dve(lambda e, c=c: e.tensor_tensor(out=xT[:, c, 0:T], in0=xT[:, c, 0:T], in1=lnn[:, 0:T], op=ALU.add),
                      reads=[("xT", c), "rec"], writes=[("xT", c)])
                P.dve(lambda e, c=c: e.tensor_scalar(out=xT[:, c, 0:T], in0=xT[:, c, 0:T], scalar1=lnps[:, gcol + c:gcol + c + 1],
                                                     scalar2=lnps[:, bcol + c:bcol + c + 1], op0=ALU.mult, op1=ALU.add),
                      reads=[("xT", c), "consts"], writes=[("xT", c)])
                P.act(lambda e, c=c: e.activation(out=xb[:, c, 0:T], in_=xT[:, c, 0:T], func=AF.Copy),
                      reads=[("xT", c)], writes=[("xb", c)])

        def resid_evac(b, ct, T):
            P.dve(lambda e: e.scalar_tensor_tensor(out=xT[:, ct, 0:T], in0=xT[:, ct, 0:T], scalar=ALPHA, in1=PS[b][:, 0:T],
                                                   op0=ALU.mult, op1=ALU.add),
                  reads=[("xT", ct), ("ps", b)], writes=[("xT", ct)])
            P.act(lambda e: e.activation(out=zb[:, ct * 512:ct * 512 + T], in_=xT[:, ct, 0:T], func=AF.Copy),
                  reads=[("xT", ct)], writes=["mneg"])
            P.act(lambda e: e.activation(out=zsq[:, ct * 512:ct * 512 + T], in_=xT[:, ct, 0:T], func=AF.Square),
                  reads=[("xT", ct)], writes=["mneg"])

        def process_tile(l, kind, ti):
            samp = (kind == "s")
            T = 32 if samp else 512
            QS = 32 if samp else 128
            NS = 1 if samp else 4
            pos0 = 1024 if samp else 512 * ti
            kt, va, kit = (KTs, VAs, KiTs) if samp else (KT, VA, KiT)
            ktn, van, kitn = ("KTs", "VAs", "KiTs") if samp else ("KT", "VA", "KiT")
            x1i = NT if samp else ti

            if l == 0:
                src = xs if samp else xp
                for j in range(NS):
                    sg = stg[j % 2]
                    sgn = ("stg", j % 2)
                    r0 = j * QS if samp else 512 * ti + j * QS
                    P.dma("sp", "stg%d" % (j % 2), lambda e, sg=sg, r0=r0: e.dma_start(out=sg[0:QS, :], in_=src[r0:r0 + QS, :]),
                          writes=[sgn])
                    for half in range(2):
                        b = rr()
                        for c4 in range(4):
                            c = half * 4 + c4
                            P.pe(lambda e, c=c, c4=c4, b=b, sg=sg: e.matmul(PS[b][:, c4 * 128:c4 * 128 + QS], lhsT=sg[0:QS, c * 128:(c + 1) * 128], rhs=ident[0:QS, 0:QS], start=True, stop=True),
                                 reads=[sgn, "consts"], writes=[("ps", b)])
                        o3 = xT[:, half * 4:half * 4 + 4, j * QS:(j + 1) * QS]
                        ob = xb[:, half * 4:half * 4 + 4, j * QS:(j + 1) * QS]
                        i3 = PS[b][:, :].rearrange("p (c t) -> p c t", c=4)[:, :, 0:QS]
                        wr = [("xT", half * 4 + c4) for c4 in range(4)]
                        wrb = [("xb", half * 4 + c4) for c4 in range(4)]
                        if not _DBG.get("s0_noact"):
                            P.act(lambda e, o3=o3, i3=i3: e.activation(out=o3, in_=i3, func=AF.Copy), reads=[("ps", b)], writes=wr)
                        if not _DBG.get("s0_nodve"):
                            P.dve(lambda e, ob=ob, i3=i3: e.tensor_copy(out=ob, in_=i3), reads=[("ps", b)], writes=wrb)
            else:
                if samp:
                    P.dma("sp", "x1l", lambda e: e.dma_start(out=xT[:, :, 0:T], in_=x1[x1i].rearrange("p (c t) -> p c t", c=8)[:, :, 0:T]),
                          reads=[("x1", x1i)], writes=[("xT", c) for c in range(8)])
                else:
                    P.dma("sp", "x1l", lambda e: e.dma_start(out=arena[:, 0:4096], in_=x1[x1i]),
                          reads=[("x1", x1i)], writes=[("xT", c) for c in range(8)])
                for c in range(8):
                    evac(c, xb[:, c, 0:T], xT[:, c, 0:T], [("xT", c)], [("xb", c)])

            if _DBG["stop"] <= -3:
                return
            if samp:
                P.dma("sp", "stg0", lambda e: e.dma_start(out=stg[0][:, :].rearrange("p (n d) -> p n d", n=8),
                                                          in_=ck[l].rearrange("(n p) d -> p n d", p=128)), writes=[("stg", 0)])
                for half in range(2):
                    b = rr()
                    for n4 in range(4):
                        n = half * 4 + n4
                        P.pe(lambda e, n=n, n4=n4, b=b: e.matmul(PS[b][:, n4 * 128:(n4 + 1) * 128], lhsT=stg[0][:, n * 128:(n + 1) * 128], rhs=ident[:, :], start=True, stop=True),
                             reads=[("stg", 0), "consts"], writes=[("ps", b)])
                    evac(half, KTs[:, half * 512:half * 512 + 512], PS[b][:, :], [("ps", b)], ["KTs"])
                if _DBG["stop"] <= -2:
                    return
                P.dma("sp", "stg1", lambda e: e.dma_start(out=stg[1][:, :].rearrange("p (n d) -> p n d", n=8),
                                                          in_=cv[l].rearrange("(n p) d -> p n d", p=128)), writes=[("stg", 1)])
                P.dve(lambda e: e.tensor_copy(out=VAs[:, 0:8, :, 0:64],
                                              in_=stg[1][:, :].rearrange("p (n g d) -> p n g d", n=8, g=2)),
                      reads=[("stg", 1)], writes=["VAs"])
                if _DBG["stop"] <= -1:
                    return
                P.dma("sp", "stg0", lambda e: e.dma_start(out=stg[0][:, 0:512].rearrange("p (n d) -> p n d", n=8),
                                                          in_=cki[l].rearrange("(n p) d -> p n d", p=128)), writes=[("stg", 0)])
                for half in range(2):
                    b = rr()
                    for n4 in range(4):
                        n = half * 4 + n4
                        P.pe(lambda e, n=n, n4=n4, b=b: e.matmul(PS[b][0:64, n4 * 128:(n4 + 1) * 128], lhsT=stg[0][:, n * 64:(n + 1) * 64], rhs=ident[:, :], start=True, stop=True),
                             reads=[("stg", 0), "consts"], writes=[("ps", b)])
                    P.act(lambda e, half=half, b=b: e.activation(out=KiTs[0:64, half * 512:half * 512 + 512], in_=PS[b][0:64, :], func=AF.Copy),
                          reads=[("ps", b)], writes=["KiTs"])
                    P.dve(lambda e, half=half, b=b: e.tensor_copy(out=KiTs[64:128, half * 512:half * 512 + 512], in_=PS[b][0:64, :]),
                          reads=[("ps", b)], writes=["KiTs"])

            if _DBG["stop"] <= 0:
                return
            use_arena = (not samp)
            if use_arena:
                wstate["cap"] = wcur["n"] + 5
                if l == 0:
                    P.dma("pool", "x1s", lambda e: e.dma_start(out=x1[x1i], in_=arena[:, 0:4096]),
                          reads=[("xT", c) for c in range(8)], writes=[("x1", x1i)])
            xb_reads = [("xb", c) for c in range(8)]

            def xrhs(kc):
                return xb[:, kc, 0:T]

            gb = big[:, 0:2048].rearrange("p (c t) -> p c t", c=4)
            gc = big[:, 2048:4096].rearrange("p (c t) -> p c t", c=4)
            hh = big[:, 4096:6144].rearrange("p (c t) -> p c t", c=4)
            for s in range(3):
                wt, wres = wslot()
                for ctl in range(4):
                    ct = s * 4 + ctl
                    b = proj_tile(wt, wres, ctl, T, xrhs, 8, xb_reads)
                    evac(ct, big[:, ct * 512:ct * 512 + T], PS[b][:, 0:T], [("ps", b)], ["big"])
            if samp:
                for c in range(4):
                    P.dma("sp", "x1l", lambda e, c=c: e.dma_start(out=uT[:, c, 0:2], in_=scv[l][:, c * 128:(c + 1) * 128].rearrange("j p -> p j"),
                                                                  allow_slow_non_contiguous=True), writes=["uT"])
            elif ti == 0:
                P.pool(lambda e: e.memset(uT[:, :, 0:2], 0.0), writes=["uT"])
            else:
                P.pool(lambda e: e.tensor_copy(out=uT[:, :, 0:2], in_=uT[:, :, 512:514]), reads=["uT"], writes=["uT"])
            P.pool(lambda e: e.tensor_tensor(out=uT[:, :, 2:2 + T], in0=gc[:, :, 0:T], in1=hh[:, :, 0:T], op=ALU.mult),
                   reads=["big"], writes=["uT"])
            for c in range(4):
                w0 = cws[:, (l * 3 + 0) * 4 + c:(l * 3 + 0) * 4 + c + 1]
                w1 = cws[:, (l * 3 + 1) * 4 + c:(l * 3 + 1) * 4 + c + 1]
                w2 = cws[:, (l * 3 + 2) * 4 + c:(l * 3 + 2) * 4 + c + 1]
                yy = gc[:, c, 0:T]
                P.dve(lambda e, c=c, w0=w0, yy=yy: e.tensor_scalar(out=yy, in0=uT[:, c, 0:T], scalar1=w0, scalar2=None, op0=ALU.mult),
                      reads=["uT", "consts"], writes=["big"])
                P.dve(lambda e, c=c, w1=w1, yy=yy: e.scalar_tensor_tensor(out=yy, in0=uT[:, c, 1:1 + T], scalar=w1, in1=yy,
                                                                         op0=ALU.mult, op1=ALU.add), reads=["uT", "big"], writes=["big"])
                P.dve(lambda e, c=c, w2=w2, yy=yy: e.scalar_tensor_tensor(out=yy, in0=uT[:, c, 2:2 + T], scalar=w2, in1=yy,
                                                                         op0=ALU.mult, op1=ALU.add), reads=["uT", "big"], writes=["big"])
                P.pool(lambda e, c=c, yy=yy: e.tensor_tensor(out=convT[:, c, 0:T], in0=gb[:, c, 0:T], in1=yy, op=ALU.mult),
                       reads=["big"], writes=["convT"])
            if samp or ti == NT - 1:
                dst = sc if samp else pc
                for c in range(4):
                    P.dma("pool", "pcv", lambda e, c=c, dst=dst: e.dma_start(out=dst[l][:, c * 128:(c + 1) * 128].rearrange("j p -> p j"), in_=uT[:, c, T:T + 2],
                                                                    allow_slow_non_contiguous=True), reads=["uT"], writes=["pc_out"])

            if _DBG["stop"] <= 1:
                return
            kvst = big[:, 6144:7680].rearrange("p (n t) -> p n t", n=3)
            wt, wres = wslot()
            b = proj_tile(wt, wres, 0, T, xrhs, 8, xb_reads)
            wsc = 1.0 / (8.0 * math.sqrt(8.0))
            P.act(lambda e, b=b: e.activation(out=wsT[0:8, 0:T], in_=PS[b][0:8, 0:T], func=AF.Copy, scale=wsc),
                  reads=[("ps", b)], writes=["lnm"])
            bt = rr()
            for j in range(NS):
                P.pe(lambda e, j=j: e.matmul(PS[bt][0:QS, j * 8:j * 8 + 8], lhsT=wsT[0:8, j * QS:(j + 1) * QS], rhs=ident[0:8, 0:8], start=True, stop=True),
                     reads=["lnm", "consts"], writes=[("ps", bt)])
            P.act(lambda e: e.activation(out=sgnT[0:QS, 0:NS, :], in_=PS[bt][0:QS, 0:NS * 8].rearrange("p (n h) -> p n h", h=8),
                                         func=AF.Copy), reads=[("ps", bt)], writes=["sgnT"])
            for jj in range(4):
                if jj == 3:
                    wt, wres = wslot()
                ctl = (1 + jj) % 4
                b = proj_tile(wt, wres, ctl, T, xrhs, 8, xb_reads)
                evac(jj, qT4[:, jj, 0:T], PS[b][:, 0:T], [("ps", b)], ["qT4"], scale=0.125)
            b = proj_tile(wt, wres, 1, T, xrhs, 8, xb_reads)
            P.act(lambda e, b=b: e.activation(out=kvst[:, 0, 0:T], in_=PS[b][:, 0:T], func=AF.Copy), reads=[("ps", b)], writes=["big"])
            P.dve(lambda e, b=b: e.tensor_copy(out=kt[:, pos0:pos0 + T], in_=PS[b][:, 0:T]), reads=[("ps", b)], writes=[ktn])
            b = proj_tile(wt, wres, 2, T, xrhs, 8, xb_reads)
            P.act(lambda e, b=b: e.activation(out=kvst[:, 1, 0:T], in_=PS[b][:, 0:T], func=AF.Copy), reads=[("ps", b)], writes=["big"])
            b = proj_tile(wt, wres, 3, T, xrhs, 8, xb_reads)
            P.act(lambda e, b=b: e.activation(out=kvst[:, 2, 0:T], in_=PS[b][:, 0:T], func=AF.Copy), reads=[("ps", b)], writes=["big"])
            P.dve(lambda e, b=b: e.tensor_copy(out=kit[:, pos0:pos0 + T], in_=PS[b][:, 0:T]), reads=[("ps", b)], writes=[kitn])
            for which in range(3):
                wd = 64 if which == 2 else 128
                b = rr()
                for j in range(NS):
                    P.pe(lambda e, j=j, which=which, wd=wd, b=b: e.matmul(PS[b][0:QS, j * 128:j * 128 + wd], lhsT=kvst[0:wd, which, j * QS:(j + 1) * QS], rhs=ident[0:wd, 0:wd], start=True, stop=True),
                         reads=["big", "consts"], writes=[("ps", b)])
                sg = stg[which % 2]
                sgn = ("stg", which % 2)
                src3 = PS[b][0:QS, :].rearrange("p (n d) -> p n d", n=4)[:, 0:NS, 0:wd]
                dst3 = sg[0:QS, 0:512].rearrange("p (n d) -> p n d", n=4)[:, 0:NS, 0:wd]
                P.act(lambda e, src3=src3, dst3=dst3: e.activation(out=dst3, in_=src3, func=AF.Copy), reads=[("ps", b)], writes=[sgn])
                if which == 1:
                    if samp:
                        vdst = VAs[0:32, 8:9, :, 0:64]
                    else:
                        vdst = VA[:, 4 * ti:4 * ti + 4, :, 0:64]
                    vsrc = PS[b][0:QS, :].rearrange("p (n g d) -> p n g d", n=4, g=2)[:, 0:NS, :, :]
                    P.dve(lambda e, vdst=vdst, vsrc=vsrc: e.tensor_copy(out=vdst, in_=vsrc), reads=[("ps", b)], writes=[van])
                if samp:
                    od = [sk, sv, ski][which][l]
                    P.dma("pool", "so%d" % (which % 2), lambda e, od=od, sg=sg, wd=wd: e.dma_start(out=od, in_=sg[0:32, 0:wd]),
                          reads=[sgn], writes=["kv_out"])
                else:
                    od = [pk, pv, pki][which][l, pos0:pos0 + 512, :].rearrange("(n p) d -> p n d", p=128)
                    P.dma("pool", "so%d" % (which % 2), lambda e, od=od, dst3=dst3: e.dma_start(out=od, in_=dst3),
                          reads=[sgn], writes=["kv_out"])
            wt, wres = wslot()
            for jj in range(4):
                b = proj_tile(wt, wres, jj, T, xrhs, 8, xb_reads)
                evac(jj, qiT[:, jj, 0:T], PS[b][:, 0:T], [("ps", b)], ["qiT"])

            if _DBG["stop"] <= 2:
                return
            NW = 4 * QS

            def Lof(j):
                return 1056 if samp else pos0 + QS * (j + 1)

            def scb(j):
                if use_arena and (j % 2 == 1):
                    return arena, ARENA_RES
                return big, ["big"]

            def idx_steps(j, overlapped=False):
                L = Lof(j)
                nblk = (L + 511) // 512
                steps = []
                SC, SCR = scb(j)

                def s_diag():
                    for h in range(8):
                        P.dve(lambda e, h=h: e.tensor_scalar(out=Dh[0:QS, h, 0:QS], in0=identb[0:QS, 0:QS],
                                                             scalar1=sgnT[0:QS, j, h:h + 1], scalar2=None, op0=ALU.mult),
                              reads=["identb", "sgnT"], writes=["Dh"])
                if overlapped:
                    s_diag()
                else:
                    steps.append(s_diag)
                pairs = [(blk, hp) for blk in range(nblk) for hp in range(4)]

                SB = [(0, 1), (2, 3), (5, 6)]
                RB = [(Rh[0], ("Rh", 0), Rh[1], ("Rh", 1)), (Rh[2], ("Rh", 2), Rh[3], ("Rh", 3)),
                      (PT[0], ("PT", 0), PT[1], ("PT", 1))]

                def emit_A(q):
                    blk, hp = pairs[q]
                    c0 = blk * 512
                    wd = min(512, L - c0)
                    for h2 in range(2):
                        h = 2 * hp + h2
                        pr = 64 * h2
                        bk = SB[q % 3][h2]
                        P.pe(lambda e, h=h, pr=pr, bk=bk: e.matmul(
                            PS[bk][0:QS, 0:wd], lhsT=qiT[pr:pr + 64, h // 2, j * QS:(j + 1) * QS],
                            rhs=kit[pr:pr + 64, c0:c0 + wd], start=True, stop=True),
                            reads=["qiT", kitn], writes=[("ps", bk)])

                def emit_B(q):
                    blk, hp = pairs[q]
                    wd = min(512, L - blk * 512)
                    for h2 in range(2):
                        bk = SB[q % 3][h2]
                        rt, rn = RB[q % 3][2 * h2], RB[q % 3][2 * h2 + 1]
                        if h2 == 0 or overlapped:
                            P.act(lambda e, bk=bk, rt=rt: e.activation(out=rt[0:QS, 0:wd], in_=PS[bk][0:QS, 0:wd], func=AF.Relu),
                                  reads=[("ps", bk)], writes=[rn])
                        else:
                            P.dve(lambda e, bk=bk, rt=rt: e.tensor_scalar(out=rt[0:QS, 0:wd], in0=PS[bk][0:QS, 0:wd], scalar1=0.0,
                                                                          scalar2=None, op0=ALU.max),
                                  reads=[("ps", bk)], writes=[rn])

                def emit_C(q):
                    blk, hp = pairs[q]
                    c0 = blk * 512
                    wd = min(512, L - c0)
                    for h2 in range(2):
                        h = 2 * hp + h2
                        rt, rn = RB[q % 3][2 * h2], RB[q % 3][2 * h2 + 1]
                        P.pe(lambda e, h=h, rt=rt: e.matmul(PS[4][0:QS, 0:wd], lhsT=Dh[0:QS, h, 0:QS],
                                                            rhs=rt[0:QS, 0:wd], start=(h == 0), stop=(h == 7)),
                             reads=["Dh", rn], writes=[("ps", 4)])
                    if hp == 3:
                        P.act(lambda e: e.activation(out=SC[0:QS, c0:c0 + wd], in_=PS[4][0:QS, 0:wd], func=AF.Copy),
                              reads=[("ps", 4)], writes=SCR)

                nq = len(pairs)
                for q in range(nq + 2):
                    def s_q(q=q):
                        if q < nq:
                            emit_A(q)
                        if 2 <= q:
                            emit_C(q - 2)
                        if q < nq:
                            emit_B(q)
                    steps.append(s_q)
                return steps

            def bis_steps(j, k0=None, share=0.40, kend=None):
                L = Lof(j)
                steps = []
                SC, SCR = scb(j)
                need_bis = samp or (L > 256)
                THR = bis[0:QS, 60:61]
                if need_bis:
                    MX, MN, RNG, MID, TMP, T1 = (bis[0:QS, k:k + 1] for k in (0, 1, 2, 3, 4, 5))
                    WT = bis[0:QS, 8:8 + NIT + 2]
                    CNT = bis[0:QS, 32:32 + NIT + 1]
                    CNA = bis[0:QS, 51:52]
                    nA = (int(L * share) // 64) * 64
                    if k0 is None or nA < 64 or L - nA < 64:
                        k0 = NIT
                    La = L - nA

                    def stride0(col, n):
                        x = bis[0:QS, col:col + 1]
                        return bass.AP(x.tensor, x.offset, [list(x.ap[0]), [0, n]])
                    jd = stride0(62, L)
                    jda = stride0(62, La)
                    ja = stride0(61, nA)

                    def s_pre():
                        P.dve(lambda e: e.tensor_reduce(out=MX, in_=SC[0:QS, 0:L], axis=AX.X, op=ALU.max, apply_absolute_value=True),
                              reads=SCR, writes=["bis"])
                        if not samp:
                            P.pool(lambda e: e.memset(SC[0:64, L - 64:L], -1e30), reads=["bis"], writes=SCR)
                        P.dve(lambda e: e.tensor_scalar(out=MN, in0=MX, scalar1=-1.0, scalar2=None, op0=ALU.mult), reads=["bis"], writes=["bis"])
                        P.dve(lambda e: e.tensor_scalar(out=RNG, in0=MX, scalar1=2.0, scalar2=1e-6, op0=ALU.mult, op1=ALU.add),
                              reads=["bis"], writes=["bis"])
                        P.dve(lambda e: e.tensor_scalar(out=WT, in0=pow2[0:QS, :], scalar1=RNG, scalar2=None, op0=ALU.mult),
                              reads=["bis", "consts"], writes=["bis"])
                        P.dve(lambda e: e.tensor_tensor(out=MID, in0=MN, in1=WT[:, 0:1], op=ALU.add), reads=["bis"], writes=["b_mid"])
                        P.dve(lambda e: e.memset(CNT, 0.0), reads=["b_cnt"], writes=["b_cnt"])
                    steps.append(s_pre)
                    for k in range(NIT):
                        def s_it(k=k):
                            if k0 <= k < (NIT if kend is None else kend):
                                P.act(lambda e: e.activation(out=ja, in_=SC[0:QS, La:L], func=AF.Sign, bias=MID, scale=-1.0,
                                                             accum_out=CNA), reads=SCR + ["b_mid"], writes=["b_cna"])
                                P.dve(lambda e: e.tensor_scalar(out=jda, in0=SC[0:QS, 0:La], scalar1=MID, scalar2=0.0,
                                                                op0=ALU.is_ge, op1=ALU.add, accum_out=CNT[:, k:k + 1]),
                                      reads=SCR + ["b_mid"], writes=["b_cnt"])
                                P.dve(lambda e: e.scalar_tensor_tensor(out=T1, in0=CNT[:, k:k + 1], scalar=2.0, in1=CNA,
                                                                       op0=ALU.mult, op1=ALU.subtract), reads=["b_cnt", "b_cna"], writes=["b_tmp"])
                                P.dve(lambda e: e.tensor_scalar(out=TMP, in0=T1, scalar1=float(512 - nA), scalar2=0.5,
                                                                op0=ALU.is_ge, op1=ALU.subtract), reads=["b_tmp"], writes=["b_tmp"])
                            else:
                                P.dve(lambda e: e.tensor_scalar(out=jd, in0=SC[0:QS, 0:L], scalar1=MID, scalar2=0.0,
                                                                op0=ALU.is_ge, op1=ALU.add, accum_out=CNT[:, k:k + 1]),
                                      reads=SCR + ["b_mid"], writes=["b_cnt"])
                                P.dve(lambda e: e.tensor_scalar(out=TMP, in0=CNT[:, k:k + 1], scalar1=256.0, scalar2=0.5,
                                                                op0=ALU.is_ge, op1=ALU.subtract), reads=["b_cnt"], writes=["b_tmp"])
                            P.dve(lambda e: e.scalar_tensor_tensor(out=MID, in0=TMP, scalar=WT[:, k:k + 1], in1=MID,
                                                                   op0=ALU.mult, op1=ALU.add), reads=["b_tmp", "bis"], writes=["b_mid"])
                        steps.append(s_it)

                    def s_thr():
                        P.dve(lambda e: e.tensor_tensor(out=THR, in0=MID, in1=WT[:, NIT:NIT + 1], op=ALU.subtract),
                              reads=["b_mid", "bis"], writes=["bis"])
                    steps.append(s_thr)
                else:
                    def s_thr0():
                        P.pool(lambda e: e.memset(SC[0:64, L - 64:L], -1e30), writes=SCR)
                        P.dve(lambda e: e.memset(THR, -1e29), reads=["bis"], writes=["bis"])
                    steps.append(s_thr0)
                return steps

            def do_mask(j):
                for g in range(2):
                    P.pool(lambda e, g=g: e.tensor_copy(out=Qz[g][64 * g:64 * g + 64, :, 0:QS],
                                                        in_=qT4[64 * g:64 * g + 64, :, j * QS:(j + 1) * QS]),
                           reads=["qT4"], writes=[("Qz", g)])
                L = Lof(j)
                SC, SCR = scb(j)
                THR = bis[0:QS, 60:61]
                P.dve(lambda e: e.tensor_scalar(out=mneg[0:QS, 0:L], in0=SC[0:QS, 0:L], scalar1=THR, scalar2=MASKV,
                                                op0=ALU.is_lt, op1=ALU.mult), reads=SCR + ["bis"], writes=["mneg"])

            def att_steps(j):
                L = Lof(j)
                nst = (L + 127) // 128
                steps = []
                units = [(g, s_) for g in range(2) for s_ in range(nst)]

                def ks_of(s_):
                    return min(128, L - 128 * s_)

                TB = [5, 6, 0]
                OB = [7, 3]
                PB = [(PT[0], ("PT", 0)), (PT[1], ("PT", 1)), (Rh[0], ("Rh", 0))]

                def emit_Q(u):
                    g, s_ = units[u]
                    ks = ks_of(s_)
                    bk = TB[u % 3]
                    if samp:
                        ty = 0 if s_ == 8 else (1 if s_ == 7 else None)
                    else:
                        ty = 0 if s_ == nst - 1 else (1 if s_ == nst - 2 else None)
                    P.pe(lambda e: e.matmul(
                        PS[bk][0:ks, 0:NW], lhsT=kt[:, 128 * s_:128 * s_ + ks],
                        rhs=Qz[g][:, :, 0:QS], start=True, stop=False),
                        reads=[ktn, ("Qz", g)], writes=[("ps", bk)])
                    P.pe(lambda e: e.matmul(
                        PS[bk][0:ks, 0:NW], lhsT=mneg[0:QS, 128 * s_:128 * s_ + ks], rhs=I4[0:QS, :, 0:QS],
                        start=False, stop=(ty is None)), reads=["mneg", "I4"], writes=[("ps", bk)])
                    if ty is not None:
                        P.pe(lambda e: e.matmul(
                            PS[bk][0:ks, 0:NW], lhsT=identb[0:ks, 0:ks], rhs=biasT[0:ks, ty, g, :, 0:QS],
                            start=False, stop=True), reads=["identb", "biasT"], writes=[("ps", bk)])

                def emit_X(u):
                    g, s_ = units[u]
                    ks = ks_of(s_)
                    bk = TB[u % 3]
                    pt, ptn = PB[u % 3]
                    P.act(lambda e: e.activation(out=pt[0:ks, 0:NW], in_=PS[bk][0:ks, 0:NW], func=AF.Exp),
                          reads=[("ps", bk)], writes=[ptn])

                def emit_V(u):
                    g, s_ = units[u]
                    ks = ks_of(s_)
                    pt, ptn = PB[u % 3]
                    ob = OB[g]
                    P.pe(lambda e: e.matmul(
                        PS[ob][0:65, 0:NW], lhsT=va[0:ks, s_, g, :], rhs=pt[0:ks, 0:NW], start=(s_ == 0), stop=(s_ == nst - 1)),
                        reads=[van, ptn], writes=[("ps", ob)])

                def emit_fin(g):
                    ob = OB[g]
                    bk = TB[g]
                    P.dve(lambda e: e.reciprocal(out=rec[64:65, 0:NW], in_=PS[ob][64:65, 0:NW]), reads=[("ps", ob)], writes=["rec"])
                    P.act(lambda e: e.activation(out=tmpO[0:64, 0:NW], in_=PS[ob][0:64, 0:NW], func=AF.Copy),
                          reads=[("ps", ob)], writes=["tmpO"])
                    P.pe(lambda e: e.matmul(PS[bk][0:64, 0:NW], lhsT=onesf[64:65, 0:64], rhs=rec[64:65, 0:NW], start=True, stop=True),
                         reads=["rec", "onesf"], writes=[("ps", bk)])
                    P.dve(lambda e: e.tensor_tensor(out=attnT[64 * g:64 * g + 64, :, j * QS:(j + 1) * QS],
                                                    in0=tmpO[0:64, 0:NW].rearrange("p (h t) -> p h t", h=4),
                                                    in1=PS[bk][0:64, 0:NW].rearrange("p (h t) -> p h t", h=4), op=ALU.mult),
                          reads=["tmpO", ("ps", bk)], writes=["attnT"])

                nu = len(units)
                for u in range(nu + 2):
                    def s_u(u=u):
                        if u < nu:
                            emit_Q(u)
                        if 2 <= u:
                            emit_V(u - 2)
                        if u < nu:
                            emit_X(u)
                    steps.append(s_u)
                steps.append(lambda: emit_fin(0))
                steps.append(lambda: emit_fin(1))
                return steps

            def run_interleaved(a, b):
                na, nb = len(a), len(b)
                ia = ib = 0
                while ia < na or ib < nb:
                    if ib >= nb or (ia < na and ia * nb <= ib * na):
                        a[ia]()
                        ia += 1
                    else:
                        b[ib]()
                        ib += 1

            ATT_FRAC_IN_IDX = 0.0
            run_interleaved(idx_steps(0), [])
            prev_att = []
            for j in range(NS):
                nxt = idx_steps(j + 1, overlapped=use_arena) if j + 1 < NS else []
                pe_stream = nxt + list(prev_att)
                k0, kend = None, None
                if use_arena and len(nxt) > 0 and not prev_att and _DBG.get("assist", True):
                    nb_ = NIT + 2
                    k0 = 0
                    kend = NIT if not prev_att else max(0, (len(nxt) * nb_) // max(1, len(pe_stream)) - 2)
                run_interleaved(pe_stream, bis_steps(j, k0, kend=kend))
                do_mask(j)
                prev_att = att_steps(j)
            for _w in range(16):
                P.pe(lambda e: e.matmul(PS[1][:, :], lhsT=kt[:, 0:128], rhs=kt[:, 0:512], start=True, stop=True),
                     reads=[ktn], writes=[("ps", 1)])
            run_interleaved(prev_att, [])

            if _DBG["stop"] <= 3:
                return
            if use_arena:
                wstate["cap"] = None
                P.dma("sp", "x1l", lambda e: e.dma_start(out=arena[:, 0:4096], in_=x1[x1i]),
                      reads=[("x1", x1i)], writes=[("xT", c) for c in range(8)])

            def mixrhs(kc):
                return convT[:, kc, 0:T] if kc < 4 else attnT[:, kc - 4, 0:T]
            for s in range(2):
                wt, wres = wslot()
                for ctl in range(4):
                    ct = s * 4 + ctl
                    b = proj_tile(wt, wres, ctl, T, mixrhs, 8, ["convT", "attnT"])
                    resid_evac(b, ct, T)
            ln_apply(l, 0, T)

            if _DBG["stop"] <= 4:
                return
            for s in range(8):
                wt, wres = wslot()
                for ctl in range(4):
                    ct = s * 4 + ctl
                    b = proj_tile(wt, wres, ctl, T, xrhs, 8, xb_reads)
                    rt = rtmp[ct % 2]
                    P.act(lambda e, b=b, rt=rt: e.activation(out=rt[:, 0:T], in_=PS[b][:, 0:T], func=AF.Relu),
                          reads=[("ps", b)], writes=[("Rh", ct % 2)])
                    P.pool(lambda e, ct=ct, rt=rt: e.tensor_tensor(out=hid[:, ct * 512:ct * 512 + T], in0=rt[:, 0:T], in1=rt[:, 0:T], op=ALU.mult),
                           reads=[("Rh", ct % 2)], writes=["big"])
            for ct in range(8):
                wt, wres = wslot()
                b = proj_tile(wt, wres, 0, T, lambda kc: hid[:, kc * 512:kc * 512 + T], 32, ["big"])
                resid_evac(b, ct, T)
            ln_apply(l, 1, T)

            if _DBG["stop"] <= 5:
                return
            if l < nlayers - 1:
                if samp:
                    P.dma("pool", "x1s", lambda e: e.dma_start(out=x1[x1i].rearrange("p (c t) -> p c t", c=8)[:, :, 0:T], in_=xT[:, :, 0:T]),
                          reads=[("xT", c) for c in range(8)], writes=[("x1", x1i)])
                else:
                    P.dma("pool", "x1s", lambda e: e.dma_start(out=x1[x1i], in_=arena[:, 0:4096]),
                          reads=[("xT", c) for c in range(8)], writes=[("x1", x1i)])
            else:
                dst = ys if samp else y
                for j in range(NS):
                    sg = stg[j % 2]
                    sgn = ("stg", j % 2)
                    for half in range(2):
                        b = rr()
                        for c4 in range(4):
                            c = half * 4 + c4
                            P.pe(lambda e, c=c, c4=c4, b=b, j=j: e.matmul(PS[b][0:QS, c4 * 128:(c4 + 1) * 128], lhsT=xT[:, c, j * QS:(j + 1) * QS], rhs=ident[:, :], start=True, stop=True),
                                 reads=[("xT", c), "consts"], writes=[("ps", b)])
                        evac(half, sg[0:QS, half * 512:half * 512 + 512], PS[b][0:QS, :], [("ps", b)], [sgn])
                    r0 = j * QS if samp else 512 * ti + j * QS
                    P.dma("pool", "so%d" % (j % 2), lambda e, sg=sg, r0=r0, dst=dst: e.dma_start(out=dst[r0:r0 + QS, :], in_=sg[0:QS, :]),
                          reads=[sgn], writes=["y_out"])

        tiles = []
        for l in range(nlayers):
            if do_sample:
                tiles.append((l, "s", 0))
            for ti in range(NT):
                tiles.append((l, "p", ti))
        for (l, kind, ti) in tiles:
            for s in range(24):
                wseq.append((l, s))
        lcast = [(1, s) for s in range(24)] if (nlayers > 1 and not _DBG.get('nol1cast')) else []
        if _DBG["maxtiles"] is not None:
            tiles = tiles[:_DBG["maxtiles"]]
        for n, (l, kind, ti) in enumerate(tiles):
            if (l == 0 and n >= 2) or l > 0 or n == len([t for t in tiles if t[0] == 0]) - 1:
                for _ in range(8 if l == 0 and n < len([t for t in tiles if t[0] == 0]) - 1 else 24):
                    if lcast:
                        emit_cast(*lcast.pop(0))
            process_tile(l, kind, ti)
        P.emit()
    return nc


_NC_CACHE = {}
_DBG = {"stop": 99, "maxtiles": None, "ncast": 24}


def kernel(x_prompt, x_sample, cache_k, cache_v, cache_kidx, state_conv,
           w_in, conv_w, w_o, ln1_g, ln1_b, w_ff1, w_ff2, ln2_g, ln2_b, rel_bias):
    f = lambda a: np.ascontiguousarray(np.asarray(a, dtype=np.float32))
    x_prompt, x_sample = f(x_prompt), f(x_sample)
    B, SEQ = x_prompt.shape[0], x_prompt.shape[1]
    wall = _prep_weights(f(w_in), f(w_o), f(w_ff1), f(w_ff2))
    lnp = np.zeros((128, DEPTH * 32), np.float32)
    for l in range(DEPTH):
        for wi_, arr in enumerate([ln1_g, ln1_b, ln2_g, ln2_b]):
            lnp[:, (l * 4 + wi_) * 8:(l * 4 + wi_) * 8 + 8] = f(arr)[l].reshape(8, 128).T
    cwp = np.zeros((128, DEPTH * 12), np.float32)
    cw = f(conv_w)
    for l in range(DEPTH):
        for j in range(3):
            cwp[:, (l * 3 + j) * 4:(l * 3 + j) * 4 + 4] = cw[l, j].reshape(4, 128).T
    consts = _static_consts()
    ck, cv_, cki, scv = f(cache_k), f(cache_v), f(cache_kidx), f(state_conv)
    if SEQ not in _NC_CACHE:
        _NC_CACHE[SEQ] = build(SEQ, do_sample=not _DBG.get('nosample', False))
    nc = _NC_CACHE[SEQ]
    in_maps = []
    for c in range(B):
        m = {"xp": x_prompt[c], "xs": x_sample[c],
             "ck": np.ascontiguousarray(ck[:, c].reshape(DEPTH, 1024, 128)),
             "cv": np.ascontiguousarray(cv_[:, c].reshape(DEPTH, 1024, 128)),
             "cki": np.ascontiguousarray(cki[:, c]),
             "scv": np.ascontiguousarray(scv[:, c]),
             "wall": wall, "lnp": lnp, "cwp": cwp, "rb": f(rel_bias)}
        m.update(consts)
        in_maps.append(m)
    res = run_bass_kernel_spmd(nc, in_maps, core_ids=list(range(B)))
    R = res.results
    y = np.stack([R[c]["y"] for c in range(B)])
    ys = np.stack([R[c]["ys"] for c in range(B)])
    pk = np.stack([R[c]["pk"].reshape(DEPTH, SEQ, 2, 64) for c in range(B)], axis=1)
    pv = np.stack([R[c]["pv"].reshape(DEPTH, SEQ, 2, 64) for c in range(B)], axis=1)
    pki = np.stack([R[c]["pki"] for c in range(B)], axis=1)
    pc = np.stack([R[c]["pc"] for c in range(B)], axis=1)
    sk = np.stack([R[c]["sk"].reshape(DEPTH, 32, 2, 64) for c in range(B)], axis=1)
    sv = np.stack([R[c]["sv"].reshape(DEPTH, 32, 2, 64) for c in range(B)], axis=1)
    ski = np.stack([R[c]["ski"] for c in range(B)], axis=1)
    sc = np.stack([R[c]["sc"] for c in range(B)], axis=1)
    return (y, ys, pk, pv, pki, pc, sk, sv, ski, sc)
```
